# Optimizing a Trainium2 kernel written in Bass

```python
import jax, jax.numpy as jnp
from jax import lax
import numpy as np

D_MODEL = 1024
BATCH = 4
SEQ = 4096
DEPTH = 1
DEC_BATCH = 16
DEC_SEQ = 2048
PAST_LEN = 128

N_META = 16
D_MIX = D_MODEL
D_CONV = D_MIX // 2
CONV_WIDTH = 3
N_HEADS = 8
QK_NOPE = 64
QK_ROPE = 32
V_HEAD = 64
Q_LORA = (3 * D_MODEL) // 8
KV_LORA = D_MODEL // 4
D_FF = 2816
Q_BLOCK = 128
ROPE_BASE = 10000.0
EPS = 1e-6
D_IN = 3 * D_CONV + Q_LORA + KV_LORA + QK_ROPE

kernel_name = "hybrid_conv_mla_macaron_encoder"


def rms_norm(x, g):
    xf = x.astype(jnp.float32)
    y = xf * lax.rsqrt(jnp.mean(xf * xf, axis=-1, keepdims=True) + EPS)
    return (y * g.astype(jnp.float32)).astype(x.dtype)


def swiglu(x, w_gate, w_up, w_down):
    return (jax.nn.silu(x @ w_gate) * (x @ w_up)) @ w_down


def rope_tables(seq_len):
    pos = jnp.arange(seq_len, dtype=jnp.float32)
    inv_freq = 1.0 / (ROPE_BASE ** (jnp.arange(0, QK_ROPE, 2, dtype=jnp.float32) / QK_ROPE))
    ang = pos[:, None] * inv_freq[None, :]
    return jnp.cos(ang), jnp.sin(ang)


def apply_rope(x, cos, sin):
    xf = x.astype(jnp.float32)
    half = QK_ROPE // 2
    x1, x2 = xf[..., :half], xf[..., half:]
    shape = (1, cos.shape[0]) + (1,) * (x.ndim - 3) + (half,)
    c = cos.reshape(shape)
    s = sin.reshape(shape)
    return jnp.concatenate([x1 * c - x2 * s, x2 * c + x1 * s], axis=-1).astype(x.dtype)


def short_conv_mixer(b_gate, c_gate, h, conv_w):
    u = c_gate * h
    up = jnp.pad(u, ((0, 0), (1, 1), (0, 0)))
    y = up[:, :-2] * conv_w[0] + up[:, 1:-1] * conv_w[1] + up[:, 2:] * conv_w[2]
    return b_gate * y


def latent_attention(q_lat, kv_lat, k_rope_raw, cos, sin, q_norm, w_uq, kv_norm, w_ukv):
    bsz, seq_len, _ = q_lat.shape
    q = (rms_norm(q_lat, q_norm) @ w_uq).reshape(bsz, seq_len, N_HEADS, QK_NOPE + QK_ROPE)
    q_nope = q[..., :QK_NOPE]
    q_rope = apply_rope(q[..., QK_NOPE:], cos, sin)
    kv = (rms_norm(kv_lat, kv_norm) @ w_ukv).reshape(bsz, seq_len, N_HEADS, QK_NOPE + V_HEAD)
    k_nope = kv[..., :QK_NOPE]
    v = kv[..., QK_NOPE:]
    k_rope = apply_rope(k_rope_raw, cos, sin)

    n_blk = -(-seq_len // Q_BLOCK)
    pad = n_blk * Q_BLOCK - seq_len
    scale = (QK_NOPE + QK_ROPE) ** -0.5

    def to_blocks(t):
        t = jnp.pad(t, ((0, 0), (0, pad), (0, 0), (0, 0)))
        return jnp.moveaxis(t.reshape(bsz, n_blk, Q_BLOCK, N_HEADS, t.shape[-1]), 1, 0)

    def attend(blk):
        qn, qr = blk
        s = (jnp.einsum('bqhd,bkhd->bhqk', qn, k_nope, preferred_element_type=jnp.float32)
             + jnp.einsum('bqhd,bkd->bhqk', qr, k_rope, preferred_element_type=jnp.float32))
        p = jax.nn.softmax(s * scale, axis=-1).astype(v.dtype)
        return jnp.einsum('bhqk,bkhd->bqhd', p, v)

    o = lax.map(attend, (to_blocks(q_nope), to_blocks(q_rope)))
    o = jnp.moveaxis(o, 0, 1).reshape(bsz, n_blk * Q_BLOCK, N_HEADS, V_HEAD)[:, :seq_len]
    return o.reshape(bsz, seq_len, N_HEADS * V_HEAD)


def encoder_layer(x, cos, sin, ffn1_norm, ffn1_w_gate, ffn1_w_up, ffn1_w_down,
                  mix_norm, w_in, conv_w, q_norm, w_uq, kv_norm, w_ukv, w_out,
                  ffn2_norm, ffn2_w_gate, ffn2_w_up, ffn2_w_down):
    x = x + 0.5 * swiglu(rms_norm(x, ffn1_norm), ffn1_w_gate, ffn1_w_up, ffn1_w_down)
    z = rms_norm(x, mix_norm) @ w_in
    cuts = [D_CONV, 2 * D_CONV, 3 * D_CONV, 3 * D_CONV + Q_LORA, 3 * D_CONV + Q_LORA + KV_LORA]
    b_gate, c_gate, h, q_lat, kv_lat, k_rope_raw = jnp.split(z, cuts, axis=-1)
    y_conv = short_conv_mixer(b_gate, c_gate, h, conv_w)
    y_att = latent_attention(q_lat, kv_lat, k_rope_raw, cos, sin, q_norm, w_uq, kv_norm, w_ukv)
    x = x + jnp.concatenate([y_conv, y_att], axis=-1) @ w_out
    x = x + 0.5 * swiglu(rms_norm(x, ffn2_norm), ffn2_w_gate, ffn2_w_up, ffn2_w_down)
    return x


def trunk(x, meta_tokens, final_norm, layer_params):
    bsz = x.shape[0]
    meta = jnp.broadcast_to(meta_tokens.astype(x.dtype)[None], (bsz, N_META, D_MODEL))
    h = jnp.concatenate([meta, x], axis=1)
    cos, sin = rope_tables(h.shape[1])
    for l in range(DEPTH):
        h = encoder_layer(h, cos, sin, *[p[l] for p in layer_params])
    h = rms_norm(h, final_norm)
    return h[:, N_META:]


def setup_inputs(seed: int = 0) -> dict:
    key = jax.random.key(seed)
    ks = jax.random.split(key, 24)
    f32 = jnp.float32

    def w(k, shape, fan_in):
        return jax.random.normal(k, shape, f32) * (fan_in ** -0.5)

    def gain(k, shape):
        return 1.0 + 0.02 * jax.random.normal(k, shape, f32)

    return {
        "x_prompt": jax.random.normal(ks[0], (BATCH, SEQ, D_MODEL), f32),
        "x_sample": jax.random.normal(ks[1], (DEC_BATCH, DEC_SEQ, D_MODEL), f32),
        "meta_tokens": jax.random.normal(ks[2], (N_META, D_MODEL), f32),
        "ffn1_norm": gain(ks[3], (DEPTH, D_MODEL)),
        "ffn1_w_gate": w(ks[4], (DEPTH, D_MODEL, D_FF), D_MODEL),
        "ffn1_w_up": w(ks[5], (DEPTH, D_MODEL, D_FF), D_MODEL),
        "ffn1_w_down": w(ks[6], (DEPTH, D_FF, D_MODEL), D_FF),
        "mix_norm": gain(ks[7], (DEPTH, D_MODEL)),
        "w_in": w(ks[8], (DEPTH, D_MODEL, D_IN), D_MODEL),
        "conv_w": w(ks[9], (DEPTH, CONV_WIDTH, D_CONV), CONV_WIDTH),
        "q_norm": gain(ks[10], (DEPTH, Q_LORA)),
        "w_uq": w(ks[11], (DEPTH, Q_LORA, N_HEADS * (QK_NOPE + QK_ROPE)), Q_LORA),
        "kv_norm": gain(ks[12], (DEPTH, KV_LORA)),
        "w_ukv": w(ks[13], (DEPTH, KV_LORA, N_HEADS * (QK_NOPE + V_HEAD)), KV_LORA),
        "w_out": w(ks[14], (DEPTH, D_MIX, D_MODEL), D_MIX),
        "ffn2_norm": gain(ks[15], (DEPTH, D_MODEL)),
        "ffn2_w_gate": w(ks[16], (DEPTH, D_MODEL, D_FF), D_MODEL),
        "ffn2_w_up": w(ks[17], (DEPTH, D_MODEL, D_FF), D_MODEL),
        "ffn2_w_down": w(ks[18], (DEPTH, D_FF, D_MODEL), D_FF),
        "final_norm": gain(ks[19], (D_MODEL,)),
    }


def reference(x_prompt, x_sample, meta_tokens, ffn1_norm, ffn1_w_gate, ffn1_w_up, ffn1_w_down,
              mix_norm, w_in, conv_w, q_norm, w_uq, kv_norm, w_ukv, w_out,
              ffn2_norm, ffn2_w_gate, ffn2_w_up, ffn2_w_down, final_norm):
    layer_params = (ffn1_norm, ffn1_w_gate, ffn1_w_up, ffn1_w_down,
                    mix_norm, w_in, conv_w, q_norm, w_uq, kv_norm, w_ukv, w_out,
                    ffn2_norm, ffn2_w_gate, ffn2_w_up, ffn2_w_down)
    y_prompt = trunk(x_prompt, meta_tokens, final_norm, layer_params)
    y_sample = trunk(x_sample, meta_tokens, final_norm, layer_params)
    return (y_prompt, y_sample)
```

```python
from contextlib import ExitStack

import numpy as np
import concourse.bass as bass
import concourse.mybir as mybir
from concourse.bass_utils import run_bass_kernel_spmd

F32 = mybir.dt.float32
BF16 = mybir.dt.bfloat16
AF = mybir.ActivationFunctionType
ALU = mybir.AluOpType

D = 1024
DFF = 2816
NJ = DFF // 128
NTOK = 6144
NROW = 6272
MROW = 6144
NMETA = 16
DIN = 2208
NCONST = 36
EPS = 1e-6
SCALE = 96.0 ** -0.5
SEGW = 2050
DEBUG = False
PHASES = 5
FFN_STEPS = 5
FFN_NT = 99
MIX_NT = 99
MIX_STEPS = 9
MIX_SUB = 9
MIX_ONLY_META = False


EVLOG = None


class Sem:
    def __init__(self, nc, name):
        self.h = nc.alloc_semaphore(name)
        self.name = name
        self.n = 0

    def sig(self, ins, k=1):
        ins.then_inc(self.h, k)
        self.n += k
        if EVLOG is not None:
            EVLOG.append((str(ins.ins.engine), "inc", self.name, k))
        return self.n


class PSem:
    def __init__(self, nc, name):
        self.s = [Sem(nc, name + "0"), Sem(nc, name + "1")]

    def sig(self, par, ins):
        return self.s[par].sig(ins, 16)

    def wait_all(self, eng, par):
        W(eng, self.s[par], self.s[par].n)


def WI(ins, sem, val):
    if val > 0:
        ins._wait_ge(sem.h, val)
        if EVLOG is not None:
            EVLOG.append((str(ins.ins.engine), "wait", sem.name, val))
    return ins


def W(eng, sem, val):
    if val > 0:
        ins = eng.wait_ge(sem.h, val)
        if EVLOG is not None:
            EVLOG.append((str(ins.ins.engine), "wait", sem.name, val))


class GQ:
    def __init__(self, nc, n=12):
        self.nc = nc
        self.sems = [Sem(nc, "gq%d" % i) for i in range(n)]
        self.i = 0

    def dma(self, out, in_, **kw):
        s = self.sems[self.i % len(self.sems)]
        self.i += 1
        W(self.nc.gpsimd, s, s.n)
        s.sig(self.nc.gpsimd.dma_start(out=out, in_=in_, **kw), 16)
        return (s, s.n)

    def drain(self, eng):
        for s in self.sems:
            W(eng, s, s.n)


def WT(eng, tok):
    if tok is not None:
        W(eng, tok[0], tok[1])


FENCE = {}


def fence(nc, eng, ins):
    key = id(eng)
    if key not in FENCE or FENCE[key][0] is not nc:
        FENCE[key] = (nc, Sem(nc, "fence%d" % len([k for k in FENCE if FENCE[k][0] is nc])))
    f = FENCE[key][1]
    f.sig(ins)
    W(eng, f, f.n)


def sb(nc, st, name, shape, dt):
    return st.enter_context(nc.sbuf_tensor(name, list(shape), dt)).ap()


def ffn_phase(nc, gq, PS, ident, epst, tag, x_src, tiles, wg_d, wu_d, wd_d, g_d, dst, gF_d=None):
    tp, ps = PS
    final = gF_d is not None
    with ExitStack() as st:
        wg = sb(nc, st, tag + "wg", [128, 8, DFF], BF16)
        wu = sb(nc, st, tag + "wu", [128, 8, DFF], BF16)
        wd = sb(nc, st, tag + "wd", [128, NJ, D], BF16)
        xin = sb(nc, st, tag + "xin", [128, 4, D], F32)
        res = sb(nc, st, tag + "res", [128, 2, D], F32)
        xn = sb(nc, st, tag + "xn", [128, 4, D], BF16)
        xnT = sb(nc, st, tag + "xnT", [128, 8, 512], BF16)
        hT = sb(nc, st, tag + "hT", [128, NJ, 512], BF16)
        stmp = sb(nc, st, tag + "stmp", [128, 2, 512], F32)
        gB = sb(nc, st, tag + "gB", [128, D], F32)
        gF = sb(nc, st, tag + "gF", [128, D], F32) if final else None
        stat = sb(nc, st, tag + "stat", [128, 2, 12], F32)
        fst = sb(nc, st, tag + "fst", [128, 2, 4], F32)

        s_xl = Sem(nc, tag + "xl")
        s_sq = Sem(nc, tag + "sq")
        s_xn = Sem(nc, tag + "xn")
        s_tp = Sem(nc, tag + "tp")
        s_tc = Sem(nc, tag + "tc")
        s_g = Sem(nc, tag + "g")
        s_si = Sem(nc, tag + "si")
        s_h = Sem(nc, tag + "h")
        s_d = Sem(nc, tag + "d")
        s_rl = PSem(nc, tag + "rl")
        s_r = Sem(nc, tag + "r")
        s_fq = Sem(nc, tag + "fq")
        s_fn = Sem(nc, tag + "fn")

        tk_g = [gq.dma(gB, g_d.partition_broadcast(128))]
        if final:
            tk_g.append(gq.dma(gF, gF_d.partition_broadcast(128)))
        cgs = [(0, 6), (6, 12), (12, 17), (17, 22)]
        tk_gu = []
        for (j0, j1) in cgs:
            grp = []
            for c in range(8):
                grp.append(gq.dma(wg[:, c, j0 * 128:j1 * 128], wg_d[c * 128:(c + 1) * 128, j0 * 128:j1 * 128]))
                grp.append(gq.dma(wu[:, c, j0 * 128:j1 * 128], wu_d[c * 128:(c + 1) * 128, j0 * 128:j1 * 128]))
            tk_gu.append(grp)
        tk_wd = [gq.dma(wd[:, j, :], wd_d[j * 128:(j + 1) * 128, :]) for j in range(NJ)]
        st_tok = {}

        cnt = dict(G=0, J=0, M=0, SUB=0)
        marks = {}

        def a_pre(t):
            row0, nsub = tiles[t]
            W(nc.sync, s_xn, t)
            s_xl.sig(nc.sync.dma_start(
                out=xin[:, 0:nsub, :],
                in_=x_src[row0:row0 + nsub * 128, :].rearrange("(s p) d -> p s d", p=128)), 16)
            W(nc.scalar, s_xl, 16 * (t + 1))
            stt = stat[:, t % 2, :]
            for s in range(nsub):
                ins = nc.scalar.activation(out=xn[:, s, :], in_=xin[:, s, :], func=AF.Square,
                                           accum_out=stt[:, s:s + 1])
            fence(nc, nc.scalar, ins)
            s_sq.sig(nc.scalar.activation(out=stt[:, 4:4 + nsub], in_=stt[:, 0:nsub], func=AF.Sqrt,
                                          scale=1.0 / D, bias=epst[:, 0:1]))
            W(nc.vector, s_sq, t + 1)
            if t == 0:
                for tk in tk_g:
                    WT(nc.vector, tk)
            fence(nc, nc.vector, nc.vector.reciprocal(out=stt[:, 8:8 + nsub], in_=stt[:, 4:4 + nsub]))
            for s in range(nsub):
                ins = nc.vector.scalar_tensor_tensor(out=xn[:, s, :], in0=xin[:, s, :],
                                                     scalar=stt[:, 8 + s:9 + s], in1=gB,
                                                     op0=ALU.mult, op1=ALU.mult)
            s_xn.sig(ins)

        def a_pe(t):
            row0, nsub = tiles[t]
            W(nc.tensor, s_xn, t + 1)
            for s in range(nsub):
                g = cnt["G"]
                W(nc.tensor, s_tc, g - 1)
                bank = tp[g % 2]
                for c in range(8):
                    ins = nc.tensor.transpose(out=bank[:, c, :], in_=xn[:, s, c * 128:(c + 1) * 128],
                                              identity=ident)
                s_tp.sig(ins)
                W(nc.scalar, s_tp, g + 1)
                s_tc.sig(nc.scalar.copy(out=xnT[:, :, s * 128:(s + 1) * 128], in_=bank))
                cnt["G"] += 1
            marks[("tc", t)] = s_tc.n

        def b_gu(t):
            row0, nsub = tiles[t]
            ntok = nsub * 128
            W(nc.tensor, s_tc, marks[("tc", t)])
            for j in range(NJ):
                J = cnt["J"]
                if t == 0:
                    for gi_, (j0, j1) in enumerate(cgs):
                        if j == j0:
                            for tk in tk_gu[gi_]:
                                WT(nc.tensor, tk)
                W(nc.tensor, s_h, J - 1)
                pg = ps[:, J % 2, 0:ntok]
                pu = ps[:, 2 + J % 2, 0:ntok]
                for c in range(8):
                    nc.tensor.matmul(pg, lhsT=wg[:, c, j * 128:(j + 1) * 128], rhs=xnT[:, c, 0:ntok],
                                     start=(c == 0), stop=(c == 7))
                for c in range(8):
                    ins = nc.tensor.matmul(pu, lhsT=wu[:, c, j * 128:(j + 1) * 128], rhs=xnT[:, c, 0:ntok],
                                           start=(c == 0), stop=(c == 7))
                s_g.sig(ins)
                W(nc.scalar, s_g, J + 1)
                W(nc.scalar, s_h, J - 1)
                s_si.sig(nc.scalar.activation(out=stmp[:, J % 2, 0:ntok], in_=pg, func=AF.Silu))
                W(nc.vector, s_si, J + 1)
                s_h.sig(nc.vector.tensor_tensor(out=hT[:, j, 0:ntok], in0=stmp[:, J % 2, 0:ntok], in1=pu,
                                                op=ALU.mult))
                cnt["J"] += 1
            marks[("h", t)] = s_h.n

        def c_down(t):
            row0, nsub = tiles[t]
            W(nc.tensor, s_h, marks[("h", t)])
            if t == 0:
                for tk in tk_wd:
                    WT(nc.tensor, tk)
            for s in range(nsub):
                sub = cnt["SUB"]
                rb = res[:, sub % 2, :]
                r0 = row0 + s * 128
                WT(nc.sync, st_tok.get(sub - 2))
                s_rl.sig(sub % 2, nc.sync.dma_start(out=rb, in_=x_src[r0:r0 + 128, :]))
                for n in range(2):
                    M = cnt["M"]
                    W(nc.tensor, s_r, M - 1)
                    pd = ps[:, 4 + M % 2, :]
                    for j in range(NJ):
                        ins = nc.tensor.matmul(pd, lhsT=hT[:, j, s * 128:(s + 1) * 128],
                                               rhs=wd[:, j, n * 512:(n + 1) * 512],
                                               start=(j == 0), stop=(j == NJ - 1))
                    s_d.sig(ins)
                    W(nc.vector, s_d, M + 1)
                    if n == 0:
                        s_rl.wait_all(nc.vector, sub % 2)
                    s_r.sig(nc.vector.scalar_tensor_tensor(
                        out=rb[:, n * 512:(n + 1) * 512], in0=pd, scalar=0.5,
                        in1=rb[:, n * 512:(n + 1) * 512], op0=ALU.mult, op1=ALU.add))
                    cnt["M"] += 1
                if final:
                    ft = fst[:, sub % 2, :]
                    W(nc.scalar, s_r, cnt["M"])
                    W(nc.scalar, s_h, marks[("h", t)])
                    fence(nc, nc.scalar, nc.scalar.activation(out=stmp.rearrange("p a b -> p (a b)"), in_=rb,
                                                              func=AF.Square, accum_out=ft[:, 0:1]))
                    s_fq.sig(nc.scalar.activation(out=ft[:, 1:2], in_=ft[:, 0:1], func=AF.Sqrt,
                                                  scale=1.0 / D, bias=epst[:, 0:1]))
                    W(nc.vector, s_fq, sub + 1)
                    fence(nc, nc.vector, nc.vector.reciprocal(out=ft[:, 2:3], in_=ft[:, 1:2]))
                    s_fn.sig(nc.vector.scalar_tensor_tensor(out=rb, in0=rb, scalar=ft[:, 2:3], in1=gF,
                                                            op0=ALU.mult, op1=ALU.mult))
                    W(nc.gpsimd, s_fn, sub + 1)
                else:
                    W(nc.gpsimd, s_r, cnt["M"])
                st_tok[sub] = gq.dma(dst[r0:r0 + 128, :], rb)
                cnt["SUB"] += 1

        nt = min(len(tiles), FFN_NT)
        if FFN_STEPS < 5:
            nt = 1
            if FFN_STEPS >= 1:
                a_pre(0)
            if FFN_STEPS >= 2:
                a_pe(0)
            if FFN_STEPS >= 3:
                b_gu(0)
            if FFN_STEPS >= 4:
                c_down(0)
            for e in (nc.gpsimd, nc.sync, nc.scalar, nc.vector, nc.tensor):
                for sem in (s_xl, s_sq, s_xn, s_tp, s_tc, s_g, s_si, s_h, s_d, s_r, s_fq, s_fn):
                    W(e, sem, sem.n)
                s_rl.wait_all(e, 0)
                s_rl.wait_all(e, 1)
                gq.drain(e)
            return
        a_pre(0)
        a_pe(0)
        b_gu(0)
        for t in range(nt):
            if t + 1 < nt:
                a_pre(t + 1)
            c_down(t)
            if t + 1 < nt:
                a_pe(t + 1)
                b_gu(t + 1)
        for e in (nc.gpsimd, nc.sync, nc.scalar, nc.vector, nc.tensor):
            gq.drain(e)


def mix_phase(nc, gq, PS, ident, epst, T):
    tp, ps = PS
    tiles = [(t * 512, 4) for t in range(12)] + [(MROW, 1)]
    if MIX_ONLY_META:
        tiles = [(MROW, 1)]
    marks_tpk = {}
    with ExitStack() as st:
        win = sb(nc, st, "m_win", [128, 8, DIN], BF16)
        wuq = sb(nc, st, "m_wuq", [128, 3, 768], BF16)
        wukv = sb(nc, st, "m_wukv", [128, 2, 1024], BF16)
        gB = sb(nc, st, "m_gB", [128, D], F32)
        gqn = sb(nc, st, "m_gq", [128, 384], F32)
        gkv = sb(nc, st, "m_gkv", [128, 256], F32)
        flg = sb(nc, st, "m_flg", [128, 2], F32)
        xin = sb(nc, st, "m_xin", [128, 2, 4, D], F32)
        tokc = sb(nc, st, "m_tokc", [128, 2, 4, NCONST], F32)
        xn = sb(nc, st, "m_xn", [128, 2, 4, D], BF16)
        xnT = sb(nc, st, "m_xnT", [128, 8, 512], BF16)
        bT = sb(nc, st, "m_bT", [128, 2, 4, 512], BF16)
        uT = sb(nc, st, "m_uT", [128, 2, 4, 512], BF16)
        csb = sb(nc, st, "m_csb", [128, 2, 512], F32)
        junk = sb(nc, st, "m_junk", [128, 640], BF16)
        dmy = sb(nc, st, "m_dmy", [128, 2], BF16)
        stat = sb(nc, st, "m_stat", [128, 2, 12], F32)
        st2 = sb(nc, st, "m_st2", [128, 2, 8], F32)
        qkn = sb(nc, st, "m_qkn", [128, 2, 640], BF16)
        qkT = sb(nc, st, "m_qkT", [128, 2, 5, 128], BF16)
        Qb = sb(nc, st, "m_Qb", [128, 2, 8, 128], BF16)
        Kb = sb(nc, st, "m_Kb", [128, 2, 8, 128], BF16)
        Vb = sb(nc, st, "m_Vb", [128, 2, 8, 65], BF16)
        QT = sb(nc, st, "m_QT", [128, 2, 8, 512], BF16)
        KT = sb(nc, st, "m_KT", [128, 2, 8, 512], BF16)
        rt = sb(nc, st, "m_rt", [128, 2, 4, 8, 16], F32)
        krt = sb(nc, st, "m_krt", [128, 2, 4, 16], F32)
        krb = sb(nc, st, "m_krb", [128, 2, 32], BF16)
        q_sb = sb(nc, st, "m_qsb", [128, 2, 768], F32)
        kv_sb = sb(nc, st, "m_kvsb", [128, 2, 1024], F32)
        usave = sb(nc, st, "m_usave", [128, 4, 4], BF16)
        hal = sb(nc, st, "m_hal", [128, 4, 6], BF16)

        s_xl = PSem(nc, "m_xl")
        s_sq = Sem(nc, "m_sq")
        s_xn = Sem(nc, "m_xn")
        s_tp = Sem(nc, "m_tp")
        s_tc = Sem(nc, "m_tc")
        s_fw = Sem(nc, "m_fw")
        s_frb = [Sem(nc, "m_fr0"), Sem(nc, "m_fr1")]
        s_ubu = Sem(nc, "m_ubu")
        s_ubb = Sem(nc, "m_ubb")
        s_z = Sem(nc, "m_z")
        s_zs = Sem(nc, "m_zs")
        s_zn = Sem(nc, "m_zn")
        s_q = Sem(nc, "m_q")
        s_qa = Sem(nc, "m_qa")
        s_qd = Sem(nc, "m_qd")
        s_kv = Sem(nc, "m_kv")
        s_ka = Sem(nc, "m_ka")
        s_kd = Sem(nc, "m_kd")
        s_misc = Sem(nc, "m_misc")

        wt = [gq.dma(gB, T["mix_norm"].partition_broadcast(128)),
              gq.dma(gqn, T["q_norm"].partition_broadcast(128)),
              gq.dma(gkv, T["kv_norm"].partition_broadcast(128)),
              gq.dma(flg, T["flags"])]
        for c in range(8):
            wt.append(gq.dma(win[:, c, :], T["w_in"][c * 128:(c + 1) * 128, :]))
        for c in range(3):
            wt.append(gq.dma(wuq[:, c, :], T["w_uq"][c * 128:(c + 1) * 128, :]))
        for c in range(2):
            wt.append(gq.dma(wukv[:, c, :], T["w_ukv"][c * 128:(c + 1) * 128, :]))
        nc.gpsimd.memset(Qb, 0.0)
        nc.gpsimd.memset(Kb, 0.0)
        s_misc.sig(nc.gpsimd.memset(Vb, 1.0))
        for e in (nc.vector, nc.scalar, nc.tensor):
            for tk in wt:
                WT(e, tk)
            W(e, s_misc, 1)
        ub_tok, v_tok, qk_tok = {}, {}, {}

        TPU = [0]
        FM = [0]
        G = [0]

        def tp_group(srcs, dst):
            u = TPU[0]
            bank = tp[u % 2]
            W(nc.tensor, s_tc, u - 1)
            for k, sap in enumerate(srcs):
                ins = nc.tensor.transpose(out=bank[:, k, :], in_=sap, identity=ident)
            s_tp.sig(ins)
            W(nc.scalar, s_tp, u + 1)
            ins = nc.scalar.copy(out=dst, in_=bank[:, 0:len(srcs), :])
            s_tc.sig(ins)
            TPU[0] += 1
            return s_tc.n

        uTd = T["uT_d"].rearrange("n p c -> p n c")
        bTd = T["bT_d"].rearrange("n p c -> p n c")
        QTd = T["QT_d"].rearrange("h p c -> p h c")
        KTd = T["KT_d"].rearrange("h p c -> p h c")
        ZQ = [(ps[:, 2, :], ps[:, 3, :]), (ps[:, 0, :], ps[:, 1, :])]
        qpa, qpb = ps[:, 4, :], ps[:, 5, :]
        last_of_tile = {}
        xn_tc = {}

        def emit_load(t):
            row0, nsub = tiles[t]
            ntok = nsub * 128
            p = t % 2
            if t >= 2:
                W(nc.sync, s_qd, last_of_tile[t - 2])
                W(nc.sync, s_zn, last_of_tile[t - 2])
                W(nc.sync, s_xn, t - 1)
            s_xl.sig(p, nc.sync.dma_start(
                out=xin[:, p, 0:nsub, :],
                in_=T["x1_d"][row0:row0 + ntok, :].rearrange("(s p) d -> p s d", p=128)))
            s_xl.sig(p, nc.sync.dma_start(
                out=tokc[:, p, 0:nsub, :],
                in_=T["tokc"][row0:row0 + ntok, :].rearrange("(s p) d -> p s d", p=128)))

        def emit_norm(t):
            row0, nsub = tiles[t]
            p = t % 2
            s_xl.wait_all(nc.scalar, p)
            W(nc.scalar, s_tc, xn_tc.get(t - 2, 0))
            W(nc.vector, s_tc, xn_tc.get(t - 2, 0))
            stt = stat[:, p, :]
            for s in range(nsub):
                ins = nc.scalar.activation(out=xn[:, p, s, :], in_=xin[:, p, s, :], func=AF.Square,
                                           accum_out=stt[:, s:s + 1])
            fence(nc, nc.scalar, ins)
            s_sq.sig(nc.scalar.activation(out=stt[:, 4:4 + nsub], in_=stt[:, 0:nsub], func=AF.Sqrt,
                                          scale=1.0 / D, bias=epst[:, 0:1]))
            W(nc.vector, s_sq, t + 1)
            s_xl.wait_all(nc.vector, p)
            fence(nc, nc.vector, nc.vector.reciprocal(out=stt[:, 8:8 + nsub], in_=stt[:, 4:4 + nsub]))
            for s in range(nsub):
                ins = nc.vector.scalar_tensor_tensor(out=xn[:, p, s, :], in0=xin[:, p, s, :],
                                                     scalar=stt[:, 8 + s:9 + s], in1=gB,
                                                     op0=ALU.mult, op1=ALU.mult)
            s_xn.sig(ins)

        for t, (row0, nsub) in enumerate(tiles[:MIX_NT]):
            ntok = nsub * 128
            p = t % 2
            is_meta = (row0 == MROW)
            if t == 0:
                emit_load(0)
                emit_norm(0)
            if t + 1 < len(tiles[:MIX_NT]):
                emit_load(t + 1)
            W(nc.tensor, s_xn, t + 1)
            for s in range(nsub):
                tcn = tp_group([xn[:, p, s, c * 128:(c + 1) * 128] for c in range(8)],
                               xnT[:, :, s * 128:(s + 1) * 128])
            xn_tc[t] = tcn
            W(nc.tensor, s_tc, tcn)

            W(nc.tensor, s_ka, G[0])

            def fm_chunk(col0):
                k = FM[0]
                W(nc.tensor, s_frb[k % 2], s_frb[k % 2].n)
                bank = ps[:, k % 2, 0:ntok]
                for c in range(8):
                    ins = nc.tensor.matmul(bank, lhsT=win[:, c, col0:col0 + 128], rhs=xnT[:, c, 0:ntok],
                                           start=(c == 0), stop=(c == 7))
                s_fw.sig(ins)
                FM[0] += 1
                return bank, k

            for tk in ub_tok.get(t - 2, []):
                WT(nc.vector, tk)
                WT(nc.scalar, tk)
            for n in range(4):
                cb, kc = fm_chunk(512 + n * 128)
                hb, kh = fm_chunk(1024 + n * 128)
                W(nc.scalar, s_fw, kc + 1)
                s_frb[kc % 2].sig(nc.scalar.copy(out=csb[:, n % 2, 0:ntok], in_=cb))
                W(nc.vector, s_fw, kh + 1)
                W(nc.vector, s_frb[kc % 2], s_frb[kc % 2].n)
                ulast = nc.vector.tensor_tensor(out=uT[:, p, n, 0:ntok], in0=csb[:, n % 2, 0:ntok], in1=hb,
                                                op=ALU.mult)
                s_frb[kh % 2].sig(ulast)
            W(nc.vector, s_frb[kh % 2], s_frb[kh % 2].n)
            if t == 3:
                nc.vector.tensor_copy(out=usave[:, :, 0:1], in_=uT[:, p, :, 511:512])
            if t == 4:
                nc.vector.tensor_copy(out=usave[:, :, 1:2], in_=uT[:, p, :, 0:1])
            if is_meta:
                nc.vector.tensor_copy(out=usave[:, :, 2:3], in_=uT[:, p, :, 15:16])
            s_ubu.sig(nc.vector.tensor_copy(out=usave[:, :, 3:4], in_=uT[:, p, :, 0:1]))
            for n in range(4):
                bb, kb = fm_chunk(n * 128)
                W(nc.scalar, s_fw, kb + 1)
                s_frb[kb % 2].sig(nc.scalar.copy(out=bT[:, p, n, 0:ntok], in_=bb))
            s_ubb.sig(nc.scalar.copy(out=dmy[:, 0:1], in_=epst[:, 0:1]))
            if not is_meta:
                seg, i = t // 4, t % 4
                c0 = seg * SEGW + 1 + i * 512
                W(nc.gpsimd, s_ubu, t + 1)
                W(nc.gpsimd, s_ubb, t + 1)
                ub_tok[t] = [gq.dma(uTd[:, :, c0:c0 + 512], uT[:, p, :, :]),
                             gq.dma(bTd[:, :, row0:row0 + 512], bT[:, p, :, :])]

            for tk in qk_tok.get(t - 2, []):
                WT(nc.scalar, tk)
            def stage_A(g, s):
                q = g % 2
                tk = tokc[:, p, s, :]
                za, zb = ZQ[q]
                W(nc.tensor, s_ka, g - 1)
                if q == 1:
                    W(nc.tensor, s_frb[0], s_frb[0].n)
                    W(nc.tensor, s_frb[1], s_frb[1].n)
                for c in range(8):
                    nc.tensor.matmul(za[:, 0:384], lhsT=xnT[:, c, s * 128:(s + 1) * 128],
                                     rhs=win[:, c, 1536:1920], start=(c == 0), stop=(c == 7))
                for c in range(8):
                    ins = nc.tensor.matmul(zb[:, 0:288], lhsT=xnT[:, c, s * 128:(s + 1) * 128],
                                           rhs=win[:, c, 1920:2208], start=(c == 0), stop=(c == 7))
                s_z.sig(ins)
                W(nc.scalar, s_z, g + 1)
                s2 = st2[:, q, :]
                nc.scalar.activation(out=junk[:, 0:384], in_=za[:, 0:384], func=AF.Square, accum_out=s2[:, 0:1])
                fence(nc, nc.scalar, nc.scalar.activation(out=junk[:, 384:640], in_=zb[:, 0:256], func=AF.Square,
                                                          accum_out=s2[:, 1:2]))
                nc.scalar.activation(out=s2[:, 2:3], in_=s2[:, 0:1], func=AF.Sqrt, scale=1.0 / 384,
                                     bias=epst[:, 0:1])
                s_zs.sig(nc.scalar.activation(out=s2[:, 3:4], in_=s2[:, 1:2], func=AF.Sqrt, scale=1.0 / 256,
                                              bias=epst[:, 0:1]))
                W(nc.vector, s_zs, g + 1)
                W(nc.vector, s_tp, marks_tpk.get(g - 2, 0))
                fence(nc, nc.vector, nc.vector.reciprocal(out=s2[:, 4:6], in_=s2[:, 2:4]))
                nc.vector.scalar_tensor_tensor(out=qkn[:, q, 0:384], in0=za[:, 0:384], scalar=s2[:, 4:5],
                                               in1=gqn, op0=ALU.mult, op1=ALU.mult)
                nc.vector.scalar_tensor_tensor(out=qkn[:, q, 384:640], in0=zb[:, 0:256], scalar=s2[:, 5:6],
                                               in1=gkv, op0=ALU.mult, op1=ALU.mult)
                cs, sn = tk[:, 0:16], tk[:, 16:32]
                x1, x2 = zb[:, 256:272], zb[:, 272:288]
                kr = krt[:, q]
                nc.vector.tensor_tensor(out=kr[:, 0, :], in0=x1, in1=cs, op=ALU.mult)
                nc.vector.tensor_tensor(out=kr[:, 1, :], in0=x2, in1=sn, op=ALU.mult)
                nc.vector.tensor_tensor(out=kr[:, 2, :], in0=x2, in1=cs, op=ALU.mult)
                fence(nc, nc.vector, nc.vector.tensor_tensor(out=kr[:, 3, :], in0=x1, in1=sn, op=ALU.mult))
                nc.vector.tensor_tensor(out=krb[:, q, 0:16], in0=kr[:, 0, :], in1=kr[:, 1, :], op=ALU.subtract)
                fence(nc, nc.vector,
                      nc.vector.tensor_tensor(out=krb[:, q, 16:32], in0=kr[:, 2, :], in1=kr[:, 3, :], op=ALU.add))
                nc.vector.tensor_copy(out=Kb[:, q, :, 97:99], in_=tk[:, 34:36].unsqueeze(1).broadcast_to([128, 8, 2]))
                nc.vector.tensor_copy(out=Qb[:, q, :, 97:99], in_=tk[:, 32:34].unsqueeze(1).broadcast_to([128, 8, 2]))
                ins = nc.vector.tensor_copy(out=Kb[:, q, :, 64:96],
                                            in_=krb[:, q, :].unsqueeze(1).broadcast_to([128, 8, 32]))
                s_zn.sig(ins)

            def stage_B(g, s):
                q = g % 2
                W(nc.tensor, s_zn, g + 1)
                return tp_group([qkn[:, q, c * 128:(c + 1) * 128] for c in range(5)], qkT[:, q, :, :])

            def stage_C(g, s, tcn):
                q = g % 2
                tk = tokc[:, p, s, :]
                cs, sn = tk[:, 0:16], tk[:, 16:32]
                za, zb = ZQ[q]
                W(nc.tensor, s_tc, tcn)
                W(nc.tensor, s_qa, g)
                for c in range(3):
                    nc.tensor.matmul(qpa[:, 0:384], lhsT=qkT[:, q, c, :], rhs=wuq[:, c, 0:384],
                                     start=(c == 0), stop=(c == 2))
                for c in range(3):
                    ins = nc.tensor.matmul(qpb[:, 0:384], lhsT=qkT[:, q, c, :], rhs=wuq[:, c, 384:768],
                                           start=(c == 0), stop=(c == 2))
                s_q.sig(ins)
                for hh, kvb in enumerate((za, zb)):
                    for c in range(2):
                        ins = nc.tensor.matmul(kvb, lhsT=qkT[:, q, 3 + c, :],
                                               rhs=wukv[:, c, hh * 512:(hh + 1) * 512],
                                               start=(c == 0), stop=(c == 1))
                s_kv.sig(ins)
                W(nc.scalar, s_q, g + 1)
                W(nc.scalar, s_qd, g - 1)
                nc.scalar.copy(out=q_sb[:, q, 0:384], in_=qpa[:, 0:384])
                s_qa.sig(nc.scalar.copy(out=q_sb[:, q, 384:768], in_=qpb[:, 0:384]))
                W(nc.vector, s_qa, g + 1)
                q3 = q_sb[:, q, :].rearrange("p (h e) -> p h e", e=96)
                nc.vector.tensor_copy(out=Qb[:, q, :, 0:64], in_=q3[:, :, 0:64])
                qa, qb_ = q3[:, :, 64:80], q3[:, :, 80:96]
                cs8 = cs.unsqueeze(1).broadcast_to([128, 8, 16])
                sn8 = sn.unsqueeze(1).broadcast_to([128, 8, 16])
                r4 = rt[:, q]
                nc.vector.tensor_tensor(out=r4[:, 0], in0=qa, in1=cs8, op=ALU.mult)
                nc.vector.tensor_tensor(out=r4[:, 1], in0=qb_, in1=sn8, op=ALU.mult)
                nc.vector.tensor_tensor(out=r4[:, 2], in0=qb_, in1=cs8, op=ALU.mult)
                ins = nc.vector.tensor_tensor(out=r4[:, 3], in0=qa, in1=sn8, op=ALU.mult)
                fence(nc, nc.vector, ins)
                nc.vector.tensor_tensor(out=Qb[:, q, :, 64:80], in0=r4[:, 0], in1=r4[:, 1], op=ALU.subtract)
                s_qd.sig(nc.vector.tensor_tensor(out=Qb[:, q, :, 80:96], in0=r4[:, 2], in1=r4[:, 3], op=ALU.add))
                W(nc.scalar, s_kv, g + 1)
                W(nc.scalar, s_kd, g - 1)
                nc.scalar.copy(out=kv_sb[:, q, 0:512], in_=za)
                s_ka.sig(nc.scalar.copy(out=kv_sb[:, q, 512:1024], in_=zb))
                W(nc.vector, s_ka, g + 1)
                WT(nc.vector, v_tok.get(g - 2))
                kv3 = kv_sb[:, q, :].rearrange("p (h e) -> p h e", e=128)
                nc.vector.tensor_copy(out=Kb[:, q, :, 0:64], in_=kv3[:, :, 0:64])
                s_kd.sig(nc.vector.tensor_copy(out=Vb[:, q, :, 0:64], in_=kv3[:, :, 64:128]))
                W(nc.gpsimd, s_kd, g + 1)
                r0 = row0 + s * 128
                v_tok[g] = gq.dma(T["V_d"][r0:r0 + 128, :, :], Vb[:, q, :, :])

            def stage_D(g, s):
                q = g % 2
                W(nc.tensor, s_qd, g + 1)
                tp_group([Qb[:, q, h, :] for h in range(8)], QT[:, p, :, s * 128:(s + 1) * 128])
                W(nc.tensor, s_kd, g + 1)
                tp_group([Kb[:, q, h, :] for h in range(8)], KT[:, p, :, s * 128:(s + 1) * 128])
                marks_tpk[g] = s_tp.n

            for s0 in range(0, nsub, 2):
                ss = [s_ for s_ in (s0, s0 + 1) if s_ < nsub]
                gs = [G[0] + k for k in range(len(ss))]
                for g, s_ in zip(gs, ss):
                    stage_A(g, s_)
                for g, s_ in zip(gs, ss):
                    tcn = stage_B(g, s_)
                    stage_C(g, s_, tcn)
                for g, s_ in zip(gs, ss):
                    stage_D(g, s_)
                G[0] += len(ss)
                if s0 == 0 and t + 1 < len(tiles[:MIX_NT]):
                    emit_norm(t + 1)
            last_of_tile[t] = G[0]
            if MIX_SUB < 6:
                continue
            W(nc.gpsimd, s_tc, s_tc.n)
            qk_tok[t] = [gq.dma(KTd[:, :, row0:row0 + ntok], KT[:, p, :, 0:ntok])]
            if not is_meta:
                qk_tok[t].append(gq.dma(QTd[:, :, row0:row0 + 512], QT[:, p, :, :]))

        if MIX_STEPS < 4:
            for e in (nc.gpsimd, nc.sync, nc.scalar, nc.vector, nc.tensor):
                for sem in (s_sq, s_xn, s_tp, s_tc, s_fw, s_frb[0], s_frb[1], s_ubu, s_ubb, s_z, s_zs, s_zn, s_q, s_qa, s_qd, s_kv, s_ka,
                            s_kd, s_misc):
                    W(e, sem, sem.n)
                s_xl.wait_all(e, 0)
                s_xl.wait_all(e, 1)
                gq.drain(e)
            return
        a_, b_ = flg[:, 0:1], flg[:, 1:2]
        um = usave[:, :, 2:3]
        W(nc.vector, s_ubu, s_ubu.n)
        fence(nc, nc.vector, nc.vector.memset(hal, 0.0))
        fence(nc, nc.vector, nc.vector.tensor_scalar(out=hal[:, :, 2:3], in0=usave[:, :, 0:1], scalar1=a_,
                                                     scalar2=None, op0=ALU.mult))
        nc.vector.tensor_copy(out=hal[:, :, 0:1], in_=um)
        nc.vector.tensor_copy(out=hal[:, :, 4:5], in_=um)
        nc.vector.tensor_scalar(out=hal[:, :, 1:2], in0=usave[:, :, 1:2], scalar1=a_, scalar2=None, op0=ALU.mult)
        ins = nc.vector.scalar_tensor_tensor(out=hal[:, :, 2:3], in0=um, scalar=b_, in1=hal[:, :, 2:3],
                                             op0=ALU.mult, op1=ALU.add)
        s_misc.sig(ins)
        W(nc.gpsimd, s_misc, 2)
        cols = [0, SEGW - 1, SEGW, 2 * SEGW - 1, 2 * SEGW, 3 * SEGW - 1]
        for k, col in enumerate(cols):
            gq.dma(uTd[:, :, col:col + 1], hal[:, :, k:k + 1], allow_slow_non_contiguous=True)
        for e in (nc.gpsimd, nc.sync, nc.scalar, nc.vector, nc.tensor):
            gq.drain(e)


def attn_phase(nc, gq, PS, T, onesb):
    tp, ps = PS
    LMAX = 4096
    with ExitStack() as st:
        KT = sb(nc, st, "a_KT", [128, 8, LMAX + NMETA], BF16)
        V = sb(nc, st, "a_V", [128, LMAX // 128 + 1, 8, 65], BF16)
        QT = sb(nc, st, "a_QT", [128, 2, 8, 512], BF16)
        NSB = 2
        DEFER = 8
        SB = [ps[:, 0:2, :].rearrange("p a b -> p (a b)"), ps[:, 2:4, :].rearrange("p a b -> p (a b)")]
        bcb = tp[0].bitcast(F32).rearrange("p a b -> p (a b)")
        wob = tp[1].bitcast(F32).rearrange("p a b -> p (a b)")
        NPB = 3
        Pb = sb(nc, st, "a_P", [128, NPB, 1024], BF16)
        wc = sb(nc, st, "o_wc", [128, 4, D], BF16)
        wa = sb(nc, st, "o_wa", [64, 8, D], BF16)
        cw = sb(nc, st, "o_cw", [128, 3, 4], F32)
        uT = sb(nc, st, "o_uT", [128, 4, 514], BF16)
        bT = sb(nc, st, "o_bT", [128, 4, 512], BF16)
        x1 = sb(nc, st, "o_x1", [128, 4, D], F32)
        yc = sb(nc, st, "o_yc", [128, 4, 512], BF16)
        acc = sb(nc, st, "o_acc", [128, 2, 512], F32)
        s_ol = Sem(nc, "o_l")
        s_oc = Sem(nc, "o_c")
        s_om = Sem(nc, "o_m")
        s_or = Sem(nc, "o_r")
        wo = T["w_out"]
        owt = [gq.dma(wc[:, n, :], wo[n * 128:(n + 1) * 128, :]) for n in range(4)]
        owt.append(gq.dma(wa, wo[512:1024, :].rearrange("(h p) d -> p h d", p=64)))
        owt.append(gq.dma(cw, T["conv_wT"]))
        for tk in owt:
            WT(nc.vector, tk)
            WT(nc.tensor, tk)
        uTd = T["uT_d"].rearrange("n p c -> p n c")
        bTd = T["bT_d"].rearrange("n p c -> p n c")
        owq = []
        ost = {}
        OC = [0]

        def o_loads(tile):
            row0 = tile * 512
            c0 = (tile // 4) * SEGW + (tile % 4) * 512
            WT(nc.sync, ost.get(tile - 1))
            s_ol.sig(nc.sync.dma_start(out=uT, in_=uTd[:, :, c0:c0 + 514]), 16)
            s_ol.sig(nc.sync.dma_start(out=bT, in_=bTd[:, :, row0:row0 + 512]), 16)
            s_ol.sig(nc.sync.dma_start(
                out=x1, in_=T["x1_d"][row0:row0 + 512, :].rearrange("(s p) d -> p s d", p=128)), 16)

        def o_conv(tile, n):
            if n == 0:
                W(nc.vector, s_ol, 48 * (tile + 1))
                W(nc.vector, s_om, 8 * tile)
            k = n % 2
            fence(nc, nc.vector, nc.vector.tensor_scalar(out=acc[:, k, :], in0=uT[:, n, 1:513],
                                                         scalar1=cw[:, 1, n:n + 1], scalar2=None, op0=ALU.mult))
            fence(nc, nc.vector, nc.vector.scalar_tensor_tensor(out=acc[:, k, :], in0=uT[:, n, 0:512],
                                                                scalar=cw[:, 0, n:n + 1], in1=acc[:, k, :],
                                                                op0=ALU.mult, op1=ALU.add))
            fence(nc, nc.vector, nc.vector.scalar_tensor_tensor(out=acc[:, k, :], in0=uT[:, n, 2:514],
                                                                scalar=cw[:, 2, n:n + 1], in1=acc[:, k, :],
                                                                op0=ALU.mult, op1=ALU.add))
            s_oc.sig(nc.vector.tensor_tensor(out=yc[:, n, :], in0=acc[:, k, :], in1=bT[:, n, :], op=ALU.mult))

        def o_mm(tile, qc, s_, n2, k):
            m = OC[0]
            if k == 0:
                W(nc.tensor, s_oc, 4 * (tile + 1))
                W(nc.tensor, s_nm, 8 * (qc + 1))
                W(nc.tensor, s_or, m)
            if k < 4:
                ins = nc.tensor.matmul(wob, lhsT=yc[:, k, s_ * 128:(s_ + 1) * 128],
                                       rhs=wc[:, k, n2 * 512:(n2 + 1) * 512], start=(k == 0), stop=False)
            else:
                h = k - 4
                ins = nc.tensor.matmul(wob, lhsT=yT[:, qc % 2, h, s_ * 128:(s_ + 1) * 128],
                                       rhs=wa[:, h, n2 * 512:(n2 + 1) * 512], start=False, stop=(h == 7))
            if k < 11:
                return
            s_om.sig(ins)
            W(nc.vector, s_om, m + 1)
            xs = x1[:, s_, n2 * 512:(n2 + 1) * 512]
            s_or.sig(nc.vector.tensor_tensor(out=xs, in0=wob, in1=xs, op=ALU.add))
            OC[0] += 1
            if s_ == 3 and n2 == 1:
                row0 = tile * 512
                W(nc.gpsimd, s_or, OC[0])
                ost[tile] = gq.dma(T["x2_d"][row0:row0 + 512, :].rearrange("(s p) d -> p s d", p=128), x1)

        def schedule_out(tile, qc, at, sp):
            items = [lambda n=n: o_conv(tile, n) for n in range(4)]
            for s_ in range(4):
                for n2 in range(2):
                    for k0 in (0, 6):
                        items.append(lambda s_=s_, n2=n2, k0=k0: [o_mm(tile, qc, s_, n2, k) for k in range(k0, k0 + 6)])
            if tile + 1 < 12:
                items.append(lambda: o_loads(tile + 1))
            for k, it in enumerate(items):
                owq.append((at + k * sp, it))

        o_loads(0)
        yT = sb(nc, st, "a_yT", [64, 2, 8, 512], BF16)
        rec = sb(nc, st, "a_rec", [128, 2, 512], F32)
        rhl = sb(nc, st, "a_rhl", [128, 2, 2, 512], BF16)
        bcs = sb(nc, st, "a_bcs", [64, 2, 512], F32)

        s_kv = Sem(nc, "a_kv")
        s_kv2 = Sem(nc, "a_kv2")
        s_ql = PSem(nc, "a_ql")
        s_s = Sem(nc, "a_s")
        s_e = Sem(nc, "a_e")
        s_pv = Sem(nc, "a_pv")
        s_rc = Sem(nc, "a_rc")
        s_bc = Sem(nc, "a_bc")
        s_bs = Sem(nc, "a_bs")
        s_nm = Sem(nc, "a_nm")
        y_tok = {}

        KTd = T["KT_d"].rearrange("h p c -> p h c")
        QTd = T["QT_d"].rearrange("h p c -> p h c")
        yTd = T["yT_d"].rearrange("h p c -> p h c")
        GB = [0]
        I0 = 0
        HC0 = 0
        QC0 = 0
        for slot, (tok0, L) in enumerate([(0, 4096), (4096, 2048)]):
            nkt = L // 128 + 1
            nqt = L // 512
            W(nc.sync, s_pv, I0)
            W(nc.scalar, s_pv, I0)
            for h in range(8):
                if h % 2 == 0:
                    s_kv.sig(nc.sync.dma_start(out=KT[:, h, 0:L], in_=KTd[:, h, tok0:tok0 + L]), 16)
                else:
                    s_kv2.sig(nc.scalar.dma_start(out=KT[:, h, 0:L], in_=KTd[:, h, tok0:tok0 + L]), 16)
            s_kv.sig(nc.sync.dma_start(out=KT[:, :, L:L + NMETA], in_=KTd[:, :, MROW:MROW + NMETA]), 16)
            for k0 in range(0, nkt - 1, 8):
                s_kv.sig(nc.sync.dma_start(
                    out=V[:, k0:k0 + 8, :, :],
                    in_=T["V_d"][tok0 + k0 * 128:tok0 + (k0 + 8) * 128, :, :].rearrange(
                        "(k p) h d -> p k h d", p=128)), 16)
            s_kv.sig(nc.sync.dma_start(out=V[0:NMETA, nkt - 1, :, :], in_=T["V_d"][MROW:MROW + NMETA, :, :]), 16)
            W(nc.tensor, s_kv, s_kv.n)
            W(nc.tensor, s_kv2, s_kv2.n)

            upq = []
            for kt0 in range(0, nkt - 1, 2):
                upq.append((kt0, 2))
            upq.append((nkt - 1, 1))
            nuh = len(upq)
            blocks = [(qt, h, kt0, nk) for qt in range(nqt) for h in range(8) for (kt0, nk) in upq]
            nb = len(blocks)

            def load_q(qt):
                qc = QC0 + qt
                if qc >= 2:
                    W(nc.sync, s_pv, I0 + (qt - 1) * 8 * nuh if qt >= 1 else I0)
                s_ql.sig(qc % 2, nc.sync.dma_start(out=QT[:, qc % 2, :, :],
                                                   in_=QTd[:, :, tok0 + qt * 512:tok0 + (qt + 1) * 512]))

            def emit_S(bi):
                qt, h, kt0, nk = blocks[bi]
                kw = 128 if kt0 < nkt - 1 else NMETA
                gi = I0 + bi
                qc = QC0 + qt
                if h == 0 and kt0 == 0:
                    s_ql.wait_all(nc.tensor, qc % 2)
                    if qt + 1 < nqt:
                        load_q(qt + 1)
                sbank = SB[gi % NSB]
                for j in range(nk):
                    kt = kt0 + j
                    ins = nc.tensor.matmul(sbank[0:kw, j * 512:(j + 1) * 512],
                                           lhsT=KT[:, h, kt * 128:kt * 128 + kw],
                                           rhs=QT[:, qc % 2, h, :], start=True, stop=True)
                    if j == 0:
                        WI(ins, s_e, gi - (NSB - 1))
                s_s.sig(ins)
                ins = nc.scalar.activation(out=Pb[0:kw, gi % NPB, 0:nk * 512], in_=sbank[0:kw, 0:nk * 512],
                                           func=AF.Exp, scale=SCALE)
                WI(ins, s_s, gi + 1)
                s_e.sig(ins)

            pending = []

            def emit_PV(bi):
                qt, h, kt0, nk = blocks[bi]
                kw = 128 if kt0 < nkt - 1 else NMETA
                gi = I0 + bi
                hc = HC0 + (bi // nuh)
                qc = QC0 + qt
                ob = ps[:, 4 + hc % 2, :]
                last = (kt0 + nk == nkt)
                if kt0 == 0:
                    W(nc.tensor, s_nm, hc - 1)
                for j in range(nk):
                    kt = kt0 + j
                    ins = nc.tensor.matmul(ob[0:65, :], lhsT=V[0:kw, kt, h, :],
                                           rhs=Pb[0:kw, gi % NPB, j * 512:(j + 1) * 512],
                                           start=(kt == 0), stop=(kt == nkt - 1))
                    if j == 0:
                        WI(ins, s_e, gi + 1)
                s_pv.sig(ins)
                if last:
                    W(nc.vector, s_pv, gi + 1)
                    fence(nc, nc.vector, nc.vector.reciprocal(out=rec[64:65, hc % 2, :], in_=ob[64:65, :]))
                    fence(nc, nc.vector, nc.vector.tensor_copy(out=rhl[64:65, hc % 2, 0, :], in_=rec[64:65, hc % 2, :]))
                    s_rc.sig(nc.vector.tensor_tensor(out=rhl[64:65, hc % 2, 1, :], in0=rec[64:65, hc % 2, :],
                                                     in1=rhl[64:65, hc % 2, 0, :], op=ALU.subtract))
                    pending.append((bi + DEFER, hc, h, qt, qc))

            def emit_norm(hc, h, qt, qc):
                ob = ps[:, 4 + hc % 2, :]
                W(nc.tensor, s_rc, hc + 1)
                W(nc.tensor, s_bs, hc)
                nc.tensor.matmul(bcb[0:64, :], lhsT=onesb[64:65, 0:64], rhs=rhl[64:65, hc % 2, 0, :],
                                 start=True, stop=False)
                s_bc.sig(nc.tensor.matmul(bcb[0:64, :], lhsT=onesb[64:65, 0:64], rhs=rhl[64:65, hc % 2, 1, :],
                                          start=False, stop=True))
                W(nc.scalar, s_bc, hc + 1)
                W(nc.scalar, s_nm, hc - 1)
                s_bs.sig(nc.scalar.copy(out=bcs[:, hc % 2, :], in_=bcb[0:64, :]))
                W(nc.vector, s_bs, hc + 1)
                if h == 0:
                    W(nc.vector, s_om, 8 * max(0, qc - 1))
                s_nm.sig(nc.vector.tensor_tensor(out=yT[:, qc % 2, h, :], in0=ob[0:64, :],
                                                 in1=bcs[:, hc % 2, :], op=ALU.mult))
                if h == 7:
                    if DEBUG:
                        W(nc.gpsimd, s_nm, hc + 1)
                        gq.dma(yTd[:, :, tok0 + qt * 512:tok0 + (qt + 1) * 512], yT[:, qc % 2, :, :])
                    schedule_out(qc, qc, GB[0] + 1, 6 if qc < 7 else 3)

            if QC0 >= 2:
                W(nc.sync, s_pv, I0)
            s_ql.sig(QC0 % 2, nc.sync.dma_start(out=QT[:, QC0 % 2, :, :], in_=QTd[:, :, tok0:tok0 + 512]))
            for bi in range(min(NSB, nb)):
                emit_S(bi)
            for bi in range(nb):
                GB[0] = I0 + bi
                if bi + NSB < nb:
                    emit_S(bi + NSB)
                emit_PV(bi)
                while pending and pending[0][0] <= bi:
                    emit_norm(*pending.pop(0)[1:])
                while owq and owq[0][0] <= GB[0]:
                    owq.pop(0)[1]()
            while pending:
                emit_norm(*pending.pop(0)[1:])
            I0 += nb
            HC0 += nb // nuh
            QC0 += nqt
        while owq:
            owq.pop(0)[1]()
        for e in (nc.gpsimd, nc.sync, nc.scalar, nc.vector, nc.tensor):
            gq.drain(e)


def out_phase(nc, gq, PS, T):
    tp, ps = PS
    with ExitStack() as st:
        wc = sb(nc, st, "o_wc", [128, 4, D], BF16)
        wa = sb(nc, st, "o_wa", [64, 8, D], BF16)
        cw = sb(nc, st, "o_cw", [128, 3, 4], F32)
        uT = sb(nc, st, "o_uT", [128, 2, 4, 514], BF16)
        bT = sb(nc, st, "o_bT", [128, 2, 4, 512], BF16)
        yT = sb(nc, st, "o_yT", [64, 2, 8, 512], BF16)
        x1 = sb(nc, st, "o_x1", [128, 2, 4, D], F32)
        yc = sb(nc, st, "o_yc", [128, 2, 4, 512], BF16)
        acc = sb(nc, st, "o_acc", [128, 2, 512], F32)

        s_l = PSem(nc, "o_l")
        s_c = Sem(nc, "o_c")
        s_m = Sem(nc, "o_m")
        s_r = Sem(nc, "o_r")

        wo = T["w_out"]
        wt = [gq.dma(wc[:, n, :], wo[n * 128:(n + 1) * 128, :]) for n in range(4)]
        wt.append(gq.dma(wa, wo[512:1024, :].rearrange("(h p) d -> p h d", p=64)))
        wt.append(gq.dma(cw, T["conv_wT"]))
        for tk in wt:
            WT(nc.vector, tk)
            WT(nc.tensor, tk)
        st_tok = {}

        uTd = T["uT_d"].rearrange("n p c -> p n c")
        bTd = T["bT_d"].rearrange("n p c -> p n c")
        yTd = T["yT_d"].rearrange("h p c -> p h c")
        M = [0]
        for t in range(12):
            p = t % 2
            row0 = t * 512
            seg, i = t // 4, t % 4
            c0 = seg * SEGW + i * 512
            WT(nc.sync, st_tok.get(t - 2))
            s_l.sig(p, nc.sync.dma_start(out=uT[:, p, :, :], in_=uTd[:, :, c0:c0 + 514]))
            s_l.sig(p, nc.sync.dma_start(out=bT[:, p, :, :], in_=bTd[:, :, row0:row0 + 512]))
            s_l.sig(p, nc.sync.dma_start(out=yT[:, p, :, :], in_=yTd[:, :, row0:row0 + 512]))
            s_l.sig(p, nc.sync.dma_start(
                out=x1[:, p, :, :], in_=T["x1_d"][row0:row0 + 512, :].rearrange("(s p) d -> p s d", p=128)))
            s_l.wait_all(nc.vector, p)
            if t >= 2:
                W(nc.vector, s_m, 8 * (t - 1))
            for n0 in (0, 2):
                for k in range(2):
                    n = n0 + k
                    nc.vector.tensor_scalar(out=acc[:, k, :], in0=uT[:, p, n, 1:513], scalar1=cw[:, 1, n:n + 1],
                                            scalar2=None, op0=ALU.mult)
                for k in range(2):
                    n = n0 + k
                    nc.vector.scalar_tensor_tensor(out=acc[:, k, :], in0=uT[:, p, n, 0:512],
                                                   scalar=cw[:, 0, n:n + 1], in1=acc[:, k, :],
                                                   op0=ALU.mult, op1=ALU.add)
                for k in range(2):
                    n = n0 + k
                    nc.vector.scalar_tensor_tensor(out=acc[:, k, :], in0=uT[:, p, n, 2:514],
                                                   scalar=cw[:, 2, n:n + 1], in1=acc[:, k, :],
                                                   op0=ALU.mult, op1=ALU.add)
                for k in range(2):
                    n = n0 + k
                    ins = nc.vector.tensor_tensor(out=yc[:, p, n, :], in0=acc[:, k, :], in1=bT[:, p, n, :],
                                                  op=ALU.mult)
            s_c.sig(ins)
            W(nc.tensor, s_c, t + 1)
            s_l.wait_all(nc.tensor, p)
            for s in range(4):
                for n2 in range(2):
                    m = M[0]
                    W(nc.tensor, s_r, m - 1)
                    bank = ps[:, m % 2, :]
                    for n in range(4):
                        nc.tensor.matmul(bank, lhsT=yc[:, p, n, s * 128:(s + 1) * 128],
                                         rhs=wc[:, n, n2 * 512:(n2 + 1) * 512], start=(n == 0), stop=False)
                    for h in range(8):
                        ins = nc.tensor.matmul(bank, lhsT=yT[:, p, h, s * 128:(s + 1) * 128],
                                               rhs=wa[:, h, n2 * 512:(n2 + 1) * 512], start=False, stop=(h == 7))
                    s_m.sig(ins)
                    W(nc.vector, s_m, m + 1)
                    xs = x1[:, p, s, n2 * 512:(n2 + 1) * 512]
                    s_r.sig(nc.vector.tensor_tensor(out=xs, in0=bank, in1=xs, op=ALU.add))
                    M[0] += 1
            W(nc.gpsimd, s_r, M[0])
            st_tok[t] = gq.dma(T["x2_d"][row0:row0 + 512, :].rearrange("(s p) d -> p s d", p=128), x1[:, p, :, :])
        for e in (nc.gpsimd, nc.sync, nc.scalar, nc.vector, nc.tensor):
            gq.drain(e)


def build_nc():
    nc = bass.Bass("TRN2", target_bir_lowering=False)
    free = sorted(nc.free_semaphores)
    nc.gpsimd.sem_clear(range(free[0], free[-1] + 1))
    nc.gpsimd.dma_reset()
    nc.all_engine_barrier()
    gq = GQ(nc, 12)
    T = {}

    def din(name, shape):
        T[name] = nc.dram_tensor(name, list(shape), F32, kind="ExternalInput").ap()

    din("xin", [NROW, D])
    din("tokc", [NROW, NCONST])
    din("flags", [128, 2])
    for f in ("ffn1", "ffn2"):
        din(f + "_norm", [D])
        din(f + "_w_gate", [D, DFF])
        din(f + "_w_up", [D, DFF])
        din(f + "_w_down", [DFF, D])
    din("mix_norm", [D])
    din("w_in", [D, DIN])
    din("conv_wT", [128, 3, 4])
    din("q_norm", [384])
    din("w_uq", [384, 768])
    din("kv_norm", [256])
    din("w_ukv", [256, 1024])
    din("w_out", [D, D])
    din("final_norm", [D])
    T["y"] = nc.dram_tensor("y", [NTOK, D], F32, kind="ExternalOutput").ap()
    kind = "ExternalOutput" if DEBUG else "Internal"

    def scr(name, shape, dt):
        T[name] = nc.dram_tensor(name, list(shape), dt, kind=kind).ap()

    scr("x1_d", [NROW, D], F32)
    scr("uT_d", [4, 128, 3 * SEGW], BF16)
    scr("bT_d", [4, 128, NTOK], BF16)
    scr("QT_d", [8, 128, NTOK], BF16)
    scr("KT_d", [8, 128, NROW], BF16)
    scr("V_d", [NROW, 8, 65], BF16)
    scr("yT_d", [8, 64, NTOK], BF16)
    scr("x2_d", [NTOK, D], F32)

    tp = [nc.alloc_psum_tensor("tp%d" % i, [128, 8, 128], BF16).ap() for i in range(2)]
    ps = nc.alloc_psum_tensor("ps", [128, 6, 512], F32).ap()
    PS = (tp, ps)
    ident = nc.alloc_sbuf_tensor("ident", [128, 128], BF16).ap()
    identf = nc.alloc_sbuf_tensor("identf", [128, 128], F32).ap()
    onesf = nc.alloc_sbuf_tensor("onesf", [128, 64], F32).ap()
    onesb = nc.alloc_sbuf_tensor("onesb", [128, 64], BF16).ap()
    epst = nc.alloc_sbuf_tensor("epst", [128, 1], F32).ap()
    s0 = Sem(nc, "init")
    fence(nc, nc.gpsimd, nc.gpsimd.memset(identf, 0.0))
    fence(nc, nc.gpsimd, nc.gpsimd.affine_select(out=identf, in_=identf, pattern=[[-1, 128]],
                                                 compare_op=ALU.not_equal, fill=1.0, base=0, channel_multiplier=1))
    nc.gpsimd.tensor_copy(out=ident, in_=identf)
    nc.gpsimd.memset(onesf, 1.0)
    nc.gpsimd.memset(onesb, 1.0)
    s0.sig(nc.gpsimd.memset(epst, EPS))
    for e in (nc.sync, nc.scalar, nc.vector, nc.tensor):
        W(e, s0, 1)

    tiles13 = [(t * 512, 4) for t in range(12)] + [(MROW, 1)]
    tiles12 = [(t * 512, 4) for t in range(12)]
    ffn_phase(nc, gq, PS, ident, epst, "f1", T["xin"], tiles13, T["ffn1_w_gate"], T["ffn1_w_up"], T["ffn1_w_down"],
              T["ffn1_norm"], T["x1_d"])
    if PHASES >= 2:
        mix_phase(nc, gq, PS, ident, epst, T)
    if PHASES >= 3:
        attn_phase(nc, gq, PS, T, onesb)
    if PHASES >= 5:
        ffn_phase(nc, gq, PS, ident, epst, "f2", T["x2_d"], tiles12, T["ffn2_w_gate"], T["ffn2_w_up"], T["ffn2_w_down"],
                  T["ffn2_norm"], T["y"], gF_d=T["final_norm"])
    return nc


def _core_layout(c):
    if c < 4:
        return [("p", c)], [("s", c)]
    i = c - 4
    return [("s", 4 + 3 * i), ("s", 4 + 3 * i + 1)], [("s", 4 + 3 * i + 2)]


def _consts(c):
    f32 = np.float32
    inv_freq = (1.0 / (f32(10000.0) ** (np.arange(0, 32, 2, dtype=f32) / f32(32)))).astype(f32)
    pos = np.zeros(NROW, f32)
    tk = np.zeros((NROW, NCONST), f32)
    if c < 4:
        pos[0:4096] = 16 + np.arange(4096)
        flags = (1.0, 0.0)
    else:
        pos[0:2048] = 16 + np.arange(2048)
        pos[2048:4096] = 16 + np.arange(2048)
        tk[0:2048, 32] = 128.0
        tk[2048:4096, 33] = 128.0
        tk[0:2048, 35] = -128.0
        tk[2048:4096, 34] = -128.0
        flags = (0.0, 1.0)
    pos[4096:6144] = 16 + np.arange(2048)
    pos[MROW:MROW + NMETA] = np.arange(NMETA)
    ang = (pos[:, None] * inv_freq[None, :]).astype(f32)
    tk[:, 0:16] = np.cos(ang).astype(f32)
    tk[:, 16:32] = np.sin(ang).astype(f32)
    fl = np.zeros((128, 2), f32)
    fl[:, 0] = flags[0]
    fl[:, 1] = flags[1]
    return tk, fl


def make_in_maps(inputs):
    f32 = np.float32
    xp = np.asarray(inputs["x_prompt"], f32)
    xs = np.asarray(inputs["x_sample"], f32)
    meta = np.asarray(inputs["meta_tokens"], f32)
    shared = {}
    for f in ("ffn1", "ffn2"):
        shared[f + "_norm"] = np.ascontiguousarray(np.asarray(inputs[f + "_norm"], f32)[0])
        for w in ("_w_gate", "_w_up", "_w_down"):
            shared[f + w] = np.ascontiguousarray(np.asarray(inputs[f + w], f32)[0])
    for k in ("mix_norm", "w_in", "q_norm", "w_uq", "kv_norm", "w_ukv", "w_out"):
        shared[k] = np.ascontiguousarray(np.asarray(inputs[k], f32)[0])
    cwt = np.asarray(inputs["conv_w"], f32)[0]
    shared["conv_wT"] = np.ascontiguousarray(cwt.reshape(3, 4, 128).transpose(2, 0, 1))
    shared["final_norm"] = np.ascontiguousarray(np.asarray(inputs["final_norm"], f32))
    maps = []
    for c in range(8):
        s0, s1 = _core_layout(c)
        xin = np.zeros((NROW, D), f32)
        r = 0
        for which, idx in s0 + s1:
            src = xp[idx] if which == "p" else xs[idx]
            xin[r:r + src.shape[0]] = src
            r += src.shape[0]
        xin[MROW:MROW + NMETA] = meta
        tk, fl = _consts(c)
        m = dict(shared)
        m["xin"] = xin
        m["tokc"] = tk
        m["flags"] = fl
        maps.append(m)
    return maps


def kernel(**inputs):
    nc = build_nc()
    maps = make_in_maps(inputs)
    res = run_bass_kernel_spmd(nc, maps, core_ids=list(range(8)))
    yp = np.zeros((4, 4096, D), np.float32)
    ys = np.zeros((16, 2048, D), np.float32)
    for c in range(8):
        y = np.asarray(res.results[c]["y"], np.float32)
        s0, s1 = _core_layout(c)
        r = 0
        for which, idx in s0 + s1:
            if which == "p":
                yp[idx] = y[r:r + 4096]
                r += 4096
            else:
                ys[idx] = y[r:r + 2048]
                r += 2048
    return (yp, ys)
```

```python
from contextlib import ExitStack

import numpy as np
import concourse.bass as bass
import concourse.mybir as mybir
from concourse.bass_utils import run_bass_kernel_spmd

F32 = mybir.dt.float32
BF16 = mybir.dt.bfloat16
AF = mybir.ActivationFunctionType
ALU = mybir.AluOpType

D = 1024
DFF = 2816
NJ = DFF // 128
NTOK = 6144
NROW = 6272
MROW = 6144
NMETA = 16
DIN = 2208
NCONST = 36
EPS = 1e-6
SCALE = 96.0 ** -0.5
SEGW = 2050
DEBUG = False
PHASES = 5
FFN_STEPS = 5
FFN_NT = 99
MIX_NT = 99
MIX_STEPS = 9
MIX_SUB = 9
MIX_ONLY_META = False


EVLOG = None


class Sem:
    def __init__(self, nc, name):
        self.h = nc.alloc_semaphore(name)
        self.name = name
        self.n = 0

    def sig(self, ins, k=1):
        ins.then_inc(self.h, k)
        self.n += k
        if EVLOG is not None:
            EVLOG.append((str(ins.ins.engine), "inc", self.name, k))
        return self.n


class PSem:
    def __init__(self, nc, name):
        self.s = [Sem(nc, name + "0"), Sem(nc, name + "1")]

    def sig(self, par, ins):
        return self.s[par].sig(ins, 16)

    def wait_all(self, eng, par):
        W(eng, self.s[par], self.s[par].n)


def WI(ins, sem, val):
    if val > 0:
        ins._wait_ge(sem.h, val)
        if EVLOG is not None:
            EVLOG.append((str(ins.ins.engine), "wait", sem.name, val))
    return ins


def W(eng, sem, val):
    if val > 0:
        ins = eng.wait_ge(sem.h, val)
        if EVLOG is not None:
            EVLOG.append((str(ins.ins.engine), "wait", sem.name, val))


class GQ:
    def __init__(self, nc, n=12):
        self.nc = nc
        self.sems = [Sem(nc, "gq%d" % i) for i in range(n)]
        self.i = 0

    def dma(self, out, in_, **kw):
        s = self.sems[self.i % len(self.sems)]
        self.i += 1
        W(self.nc.gpsimd, s, s.n)
        s.sig(self.nc.gpsimd.dma_start(out=out, in_=in_, **kw), 16)
        return (s, s.n)

    def drain(self, eng):
        for s in self.sems:
            W(eng, s, s.n)


def WT(eng, tok):
    if tok is not None:
        W(eng, tok[0], tok[1])


FENCE = {}


def fence(nc, eng, ins):
    key = id(eng)
    if key not in FENCE or FENCE[key][0] is not nc:
        FENCE[key] = (nc, Sem(nc, "fence%d" % len([k for k in FENCE if FENCE[k][0] is nc])))
    f = FENCE[key][1]
    f.sig(ins)
    W(eng, f, f.n)


def sb(nc, st, name, shape, dt):
    return st.enter_context(nc.sbuf_tensor(name, list(shape), dt)).ap()


def ffn_phase(nc, gq, PS, ident, epst, tag, x_src, tiles, wg_d, wu_d, wd_d, g_d, dst, gF_d=None):
    tp, ps = PS
    final = gF_d is not None
    with ExitStack() as st:
        wg = sb(nc, st, tag + "wg", [128, 8, DFF], BF16)
        wu = sb(nc, st, tag + "wu", [128, 8, DFF], BF16)
        wd = sb(nc, st, tag + "wd", [128, NJ, D], BF16)
        xin = sb(nc, st, tag + "xin", [128, 4, D], F32)
        res = sb(nc, st, tag + "res", [128, 2, D], F32)
        xn = sb(nc, st, tag + "xn", [128, 4, D], BF16)
        xnT = sb(nc, st, tag + "xnT", [128, 8, 512], BF16)
        hT = sb(nc, st, tag + "hT", [128, NJ, 512], BF16)
        stmp = sb(nc, st, tag + "stmp", [128, 2, 512], F32)
        gB = sb(nc, st, tag + "gB", [128, D], F32)
        gF = sb(nc, st, tag + "gF", [128, D], F32) if final else None
        stat = sb(nc, st, tag + "stat", [128, 2, 12], F32)
        fst = sb(nc, st, tag + "fst", [128, 2, 4], F32)

        s_xl = Sem(nc, tag + "xl")
        s_sq = Sem(nc, tag + "sq")
        s_xn = Sem(nc, tag + "xn")
        s_tp = Sem(nc, tag + "tp")
        s_tc = Sem(nc, tag + "tc")
        s_g = Sem(nc, tag + "g")
        s_si = Sem(nc, tag + "si")
        s_h = Sem(nc, tag + "h")
        s_d = Sem(nc, tag + "d")
        s_rl = PSem(nc, tag + "rl")
        s_r = Sem(nc, tag + "r")
        s_fq = Sem(nc, tag + "fq")
        s_fn = Sem(nc, tag + "fn")

        tk_g = [gq.dma(gB, g_d.partition_broadcast(128))]
        if final:
            tk_g.append(gq.dma(gF, gF_d.partition_broadcast(128)))
        cgs = [(0, 6), (6, 12), (12, 17), (17, 22)]
        tk_gu = []
        for (j0, j1) in cgs:
            grp = []
            for c in range(8):
                grp.append(gq.dma(wg[:, c, j0 * 128:j1 * 128], wg_d[c * 128:(c + 1) * 128, j0 * 128:j1 * 128]))
                grp.append(gq.dma(wu[:, c, j0 * 128:j1 * 128], wu_d[c * 128:(c + 1) * 128, j0 * 128:j1 * 128]))
            tk_gu.append(grp)
        tk_wd = [gq.dma(wd[:, j, :], wd_d[j * 128:(j + 1) * 128, :]) for j in range(NJ)]
        st_tok = {}

        cnt = dict(G=0, J=0, M=0, SUB=0)
        marks = {}

        def a_pre(t):
            row0, nsub = tiles[t]
            W(nc.sync, s_xn, t)
            s_xl.sig(nc.sync.dma_start(
                out=xin[:, 0:nsub, :],
                in_=x_src[row0:row0 + nsub * 128, :].rearrange("(s p) d -> p s d", p=128)), 16)
            W(nc.scalar, s_xl, 16 * (t + 1))
            stt = stat[:, t % 2, :]
            for s in range(nsub):
                ins = nc.scalar.activation(out=xn[:, s, :], in_=xin[:, s, :], func=AF.Square,
                                           accum_out=stt[:, s:s + 1])
            fence(nc, nc.scalar, ins)
            s_sq.sig(nc.scalar.activation(out=stt[:, 4:4 + nsub], in_=stt[:, 0:nsub], func=AF.Sqrt,
                                          scale=1.0 / D, bias=epst[:, 0:1]))
            W(nc.vector, s_sq, t + 1)
            if t == 0:
                for tk in tk_g:
                    WT(nc.vector, tk)
            fence(nc, nc.vector, nc.vector.reciprocal(out=stt[:, 8:8 + nsub], in_=stt[:, 4:4 + nsub]))
            for s in range(nsub):
                ins = nc.vector.scalar_tensor_tensor(out=xn[:, s, :], in0=xin[:, s, :],
                                                     scalar=stt[:, 8 + s:9 + s], in1=gB,
                                                     op0=ALU.mult, op1=ALU.mult)
            s_xn.sig(ins)

        def a_pe(t):
            row0, nsub = tiles[t]
            W(nc.tensor, s_xn, t + 1)
            for s in range(nsub):
                g = cnt["G"]
                W(nc.tensor, s_tc, g - 1)
                bank = tp[g % 2]
                for c in range(8):
                    ins = nc.tensor.transpose(out=bank[:, c, :], in_=xn[:, s, c * 128:(c + 1) * 128],
                                              identity=ident)
                s_tp.sig(ins)
                W(nc.scalar, s_tp, g + 1)
                s_tc.sig(nc.scalar.copy(out=xnT[:, :, s * 128:(s + 1) * 128], in_=bank))
                cnt["G"] += 1
            marks[("tc", t)] = s_tc.n

        def b_gu(t):
            row0, nsub = tiles[t]
            ntok = nsub * 128
            W(nc.tensor, s_tc, marks[("tc", t)])
            for j in range(NJ):
                J = cnt["J"]
                if t == 0:
                    for gi_, (j0, j1) in enumerate(cgs):
                        if j == j0:
                            for tk in tk_gu[gi_]:
                                WT(nc.tensor, tk)
                W(nc.tensor, s_h, J - 1)
                pg = ps[:, J % 2, 0:ntok]
                pu = ps[:, 2 + J % 2, 0:ntok]
                for c in range(8):
                    nc.tensor.matmul(pg, lhsT=wg[:, c, j * 128:(j + 1) * 128], rhs=xnT[:, c, 0:ntok],
                                     start=(c == 0), stop=(c == 7))
                for c in range(8):
                    ins = nc.tensor.matmul(pu, lhsT=wu[:, c, j * 128:(j + 1) * 128], rhs=xnT[:, c, 0:ntok],
                                           start=(c == 0), stop=(c == 7))
                s_g.sig(ins)
                W(nc.scalar, s_g, J + 1)
                W(nc.scalar, s_h, J - 1)
                s_si.sig(nc.scalar.activation(out=stmp[:, J % 2, 0:ntok], in_=pg, func=AF.Silu))
                W(nc.vector, s_si, J + 1)
                s_h.sig(nc.vector.tensor_tensor(out=hT[:, j, 0:ntok], in0=stmp[:, J % 2, 0:ntok], in1=pu,
                                                op=ALU.mult))
                cnt["J"] += 1
            marks[("h", t)] = s_h.n

        def c_down(t):
            row0, nsub = tiles[t]
            W(nc.tensor, s_h, marks[("h", t)])
            if t == 0:
                for tk in tk_wd:
                    WT(nc.tensor, tk)
            for s in range(nsub):
                sub = cnt["SUB"]
                rb = res[:, sub % 2, :]
                r0 = row0 + s * 128
                WT(nc.sync, st_tok.get(sub - 2))
                s_rl.sig(sub % 2, nc.sync.dma_start(out=rb, in_=x_src[r0:r0 + 128, :]))
                for n in range(2):
                    M = cnt["M"]
                    W(nc.tensor, s_r, M - 1)
                    pd = ps[:, 4 + M % 2, :]
                    for j in range(NJ):
                        ins = nc.tensor.matmul(pd, lhsT=hT[:, j, s * 128:(s + 1) * 128],
                                               rhs=wd[:, j, n * 512:(n + 1) * 512],
                                               start=(j == 0), stop=(j == NJ - 1))
                    s_d.sig(ins)
                    W(nc.vector, s_d, M + 1)
                    if n == 0:
                        s_rl.wait_all(nc.vector, sub % 2)
                    s_r.sig(nc.vector.scalar_tensor_tensor(
                        out=rb[:, n * 512:(n + 1) * 512], in0=pd, scalar=0.5,
                        in1=rb[:, n * 512:(n + 1) * 512], op0=ALU.mult, op1=ALU.add))
                    cnt["M"] += 1
                if final:
                    ft = fst[:, sub % 2, :]
                    W(nc.scalar, s_r, cnt["M"])
                    W(nc.scalar, s_h, marks[("h", t)])
                    fence(nc, nc.scalar, nc.scalar.activation(out=stmp.rearrange("p a b -> p (a b)"), in_=rb,
                                                              func=AF.Square, accum_out=ft[:, 0:1]))
                    s_fq.sig(nc.scalar.activation(out=ft[:, 1:2], in_=ft[:, 0:1], func=AF.Sqrt,
                                                  scale=1.0 / D, bias=epst[:, 0:1]))
                    W(nc.vector, s_fq, sub + 1)
                    fence(nc, nc.vector, nc.vector.reciprocal(out=ft[:, 2:3], in_=ft[:, 1:2]))
                    s_fn.sig(nc.vector.scalar_tensor_tensor(out=rb, in0=rb, scalar=ft[:, 2:3], in1=gF,
                                                            op0=ALU.mult, op1=ALU.mult))
                    W(nc.gpsimd, s_fn, sub + 1)
                else:
                    W(nc.gpsimd, s_r, cnt["M"])
                st_tok[sub] = gq.dma(dst[r0:r0 + 128, :], rb)
                cnt["SUB"] += 1

        nt = min(len(tiles), FFN_NT)
        if FFN_STEPS < 5:
            nt = 1
            if FFN_STEPS >= 1:
                a_pre(0)
            if FFN_STEPS >= 2:
                a_pe(0)
            if FFN_STEPS >= 3:
                b_gu(0)
            if FFN_STEPS >= 4:
                c_down(0)
            for e in (nc.gpsimd, nc.sync, nc.scalar, nc.vector, nc.tensor):
                for sem in (s_xl, s_sq, s_xn, s_tp, s_tc, s_g, s_si, s_h, s_d, s_r, s_fq, s_fn):
                    W(e, sem, sem.n)
                s_rl.wait_all(e, 0)
                s_rl.wait_all(e, 1)
                gq.drain(e)
            return
        a_pre(0)
        a_pe(0)
        b_gu(0)
        for t in range(nt):
            if t + 1 < nt:
                a_pre(t + 1)
            c_down(t)
            if t + 1 < nt:
                a_pe(t + 1)
                b_gu(t + 1)
        for e in (nc.gpsimd, nc.sync, nc.scalar, nc.vector, nc.tensor):
            gq.drain(e)


def mix_phase(nc, gq, PS, ident, epst, T):
    tp, ps = PS
    tiles = [(t * 512, 4) for t in range(12)] + [(MROW, 1)]
    if MIX_ONLY_META:
        tiles = [(MROW, 1)]
    marks_tpk = {}
    with ExitStack() as st:
        win = sb(nc, st, "m_win", [128, 8, DIN], BF16)
        wuq = sb(nc, st, "m_wuq", [128, 3, 768], BF16)
        wukv = sb(nc, st, "m_wukv", [128, 2, 1024], BF16)
        gB = sb(nc, st, "m_gB", [128, D], F32)
        gqn = sb(nc, st, "m_gq", [128, 384], F32)
        gkv = sb(nc, st, "m_gkv", [128, 256], F32)
        flg = sb(nc, st, "m_flg", [128, 2], F32)
        xin = sb(nc, st, "m_xin", [128, 2, 4, D], F32)
        tokc = sb(nc, st, "m_tokc", [128, 2, 4, NCONST], F32)
        xn = sb(nc, st, "m_xn", [128, 2, 4, D], BF16)
        xnT = sb(nc, st, "m_xnT", [128, 8, 512], BF16)
        bT = sb(nc, st, "m_bT", [128, 2, 4, 512], BF16)
        uT = sb(nc, st, "m_uT", [128, 2, 4, 512], BF16)
        csb = sb(nc, st, "m_csb", [128, 2, 512], F32)
        junk = sb(nc, st, "m_junk", [128, 640], BF16)
        dmy = sb(nc, st, "m_dmy", [128, 2], BF16)
        stat = sb(nc, st, "m_stat", [128, 2, 12], F32)
        st2 = sb(nc, st, "m_st2", [128, 2, 8], F32)
        qkn = sb(nc, st, "m_qkn", [128, 2, 640], BF16)
        qkT = sb(nc, st, "m_qkT", [128, 2, 5, 128], BF16)
        Qb = sb(nc, st, "m_Qb", [128, 2, 8, 128], BF16)
        Kb = sb(nc, st, "m_Kb", [128, 2, 8, 128], BF16)
        Vb = sb(nc, st, "m_Vb", [128, 2, 8, 65], BF16)
        QT = sb(nc, st, "m_QT", [128, 2, 8, 512], BF16)
        KT = sb(nc, st, "m_KT", [128, 2, 8, 512], BF16)
        rt = sb(nc, st, "m_rt", [128, 2, 4, 8, 16], F32)
        krt = sb(nc, st, "m_krt", [128, 2, 4, 16], F32)
        krb = sb(nc, st, "m_krb", [128, 2, 32], BF16)
        q_sb = sb(nc, st, "m_qsb", [128, 2, 768], F32)
        kv_sb = sb(nc, st, "m_kvsb", [128, 2, 1024], F32)
        usave = sb(nc, st, "m_usave", [128, 4, 4], BF16)
        hal = sb(nc, st, "m_hal", [128, 4, 6], BF16)

        s_xl = PSem(nc, "m_xl")
        s_sq = Sem(nc, "m_sq")
        s_xn = Sem(nc, "m_xn")
        s_tp = Sem(nc, "m_tp")
        s_tc = Sem(nc, "m_tc")
        s_fw = Sem(nc, "m_fw")
        s_frb = [Sem(nc, "m_fr0"), Sem(nc, "m_fr1")]
        s_ubu = Sem(nc, "m_ubu")
        s_ubb = Sem(nc, "m_ubb")
        s_z = Sem(nc, "m_z")
        s_zs = Sem(nc, "m_zs")
        s_zn = Sem(nc, "m_zn")
        s_q = Sem(nc, "m_q")
        s_qa = Sem(nc, "m_qa")
        s_qd = Sem(nc, "m_qd")
        s_kv = Sem(nc, "m_kv")
        s_ka = Sem(nc, "m_ka")
        s_kd = Sem(nc, "m_kd")
        s_misc = Sem(nc, "m_misc")

        wt = [gq.dma(gB, T["mix_norm"].partition_broadcast(128)),
              gq.dma(gqn, T["q_norm"].partition_broadcast(128)),
              gq.dma(gkv, T["kv_norm"].partition_broadcast(128)),
              gq.dma(flg, T["flags"])]
        for c in range(8):
            wt.append(gq.dma(win[:, c, :], T["w_in"][c * 128:(c + 1) * 128, :]))
        for c in range(3):
            wt.append(gq.dma(wuq[:, c, :], T["w_uq"][c * 128:(c + 1) * 128, :]))
        for c in range(2):
            wt.append(gq.dma(wukv[:, c, :], T["w_ukv"][c * 128:(c + 1) * 128, :]))
        nc.gpsimd.memset(Qb, 0.0)
        nc.gpsimd.memset(Kb, 0.0)
        s_misc.sig(nc.gpsimd.memset(Vb, 1.0))
        WT(nc.vector, wt[0])
        ub_tok, v_tok, qk_tok = {}, {}, {}

        TPU = [0]
        FM = [0]
        G = [0]

        def tp_group(srcs, dst):
            u = TPU[0]
            bank = tp[u % 2]
            W(nc.tensor, s_tc, u - 1)
            for k, sap in enumerate(srcs):
                ins = nc.tensor.transpose(out=bank[:, k, :], in_=sap, identity=ident)
            s_tp.sig(ins)
            W(nc.scalar, s_tp, u + 1)
            ins = nc.scalar.copy(out=dst, in_=bank[:, 0:len(srcs), :])
            s_tc.sig(ins)
            TPU[0] += 1
            return s_tc.n

        uTd = T["uT_d"].rearrange("n p c -> p n c")
        bTd = T["bT_d"].rearrange("n p c -> p n c")
        QTd = T["QT_d"].rearrange("h p c -> p h c")
        KTd = T["KT_d"].rearrange("h p c -> p h c")
        ZQ = [(ps[:, 2, :], ps[:, 3, :]), (ps[:, 0, :], ps[:, 1, :])]
        qpa, qpb = ps[:, 4, :], ps[:, 5, :]
        last_of_tile = {}
        xn_tc = {}

        def emit_load(t):
            row0, nsub = tiles[t]
            ntok = nsub * 128
            p = t % 2
            if t >= 2:
                W(nc.sync, s_qd, last_of_tile[t - 2])
                W(nc.sync, s_zn, last_of_tile[t - 2])
                W(nc.sync, s_xn, t - 1)
            s_xl.sig(p, nc.sync.dma_start(
                out=xin[:, p, 0:nsub, :],
                in_=T["x1_d"][row0:row0 + ntok, :].rearrange("(s p) d -> p s d", p=128)))
            s_xl.sig(p, nc.sync.dma_start(
                out=tokc[:, p, 0:nsub, :],
                in_=T["tokc"][row0:row0 + ntok, :].rearrange("(s p) d -> p s d", p=128)))

        def emit_norm(t):
            row0, nsub = tiles[t]
            p = t % 2
            s_xl.wait_all(nc.scalar, p)
            W(nc.scalar, s_tc, xn_tc.get(t - 2, 0))
            W(nc.vector, s_tc, xn_tc.get(t - 2, 0))
            stt = stat[:, p, :]
            for s in range(nsub):
                ins = nc.scalar.activation(out=xn[:, p, s, :], in_=xin[:, p, s, :], func=AF.Square,
                                           accum_out=stt[:, s:s + 1])
            fence(nc, nc.scalar, ins)
            s_sq.sig(nc.scalar.activation(out=stt[:, 4:4 + nsub], in_=stt[:, 0:nsub], func=AF.Sqrt,
                                          scale=1.0 / D, bias=epst[:, 0:1]))
            W(nc.vector, s_sq, t + 1)
            s_xl.wait_all(nc.vector, p)
            fence(nc, nc.vector, nc.vector.reciprocal(out=stt[:, 8:8 + nsub], in_=stt[:, 4:4 + nsub]))
            for s in range(nsub):
                ins = nc.vector.scalar_tensor_tensor(out=xn[:, p, s, :], in0=xin[:, p, s, :],
                                                     scalar=stt[:, 8 + s:9 + s], in1=gB,
                                                     op0=ALU.mult, op1=ALU.mult)
            s_xn.sig(ins)

        for t, (row0, nsub) in enumerate(tiles[:MIX_NT]):
            ntok = nsub * 128
            p = t % 2
            is_meta = (row0 == MROW)
            if t == 0:
                emit_load(0)
                emit_norm(0)
            if t + 1 < len(tiles[:MIX_NT]):
                emit_load(t + 1)
            W(nc.tensor, s_xn, t + 1)
            for s in range(nsub):
                tcn = tp_group([xn[:, p, s, c * 128:(c + 1) * 128] for c in range(8)],
                               xnT[:, :, s * 128:(s + 1) * 128])
            xn_tc[t] = tcn
            W(nc.tensor, s_tc, tcn)
            if t == 0:
                for e in (nc.vector, nc.scalar, nc.tensor):
                    for tk in wt:
                        WT(e, tk)
                    W(e, s_misc, 1)

            W(nc.tensor, s_ka, G[0])

            def fm_chunk(col0):
                k = FM[0]
                W(nc.tensor, s_frb[k % 2], s_frb[k % 2].n)
                bank = ps[:, k % 2, 0:ntok]
                for c in range(8):
                    ins = nc.tensor.matmul(bank, lhsT=win[:, c, col0:col0 + 128], rhs=xnT[:, c, 0:ntok],
                                           start=(c == 0), stop=(c == 7))
                s_fw.sig(ins)
                FM[0] += 1
                return bank, k

            for tk in ub_tok.get(t - 2, []):
                WT(nc.vector, tk)
                WT(nc.scalar, tk)
            for n in range(4):
                cb, kc = fm_chunk(512 + n * 128)
                hb, kh = fm_chunk(1024 + n * 128)
                W(nc.scalar, s_fw, kc + 1)
                s_frb[kc % 2].sig(nc.scalar.copy(out=csb[:, n % 2, 0:ntok], in_=cb))
                W(nc.vector, s_fw, kh + 1)
                W(nc.vector, s_frb[kc % 2], s_frb[kc % 2].n)
                ulast = nc.vector.tensor_tensor(out=uT[:, p, n, 0:ntok], in0=csb[:, n % 2, 0:ntok], in1=hb,
                                                op=ALU.mult)
                s_frb[kh % 2].sig(ulast)
            W(nc.vector, s_frb[kh % 2], s_frb[kh % 2].n)
            if t == 3:
                nc.vector.tensor_copy(out=usave[:, :, 0:1], in_=uT[:, p, :, 511:512])
            if t == 4:
                nc.vector.tensor_copy(out=usave[:, :, 1:2], in_=uT[:, p, :, 0:1])
            if is_meta:
                nc.vector.tensor_copy(out=usave[:, :, 2:3], in_=uT[:, p, :, 15:16])
            s_ubu.sig(nc.vector.tensor_copy(out=usave[:, :, 3:4], in_=uT[:, p, :, 0:1]))
            for n in range(4):
                bb, kb = fm_chunk(n * 128)
                W(nc.scalar, s_fw, kb + 1)
                s_frb[kb % 2].sig(nc.scalar.copy(out=bT[:, p, n, 0:ntok], in_=bb))
            s_ubb.sig(nc.scalar.copy(out=dmy[:, 0:1], in_=epst[:, 0:1]))
            if not is_meta:
                seg, i = t // 4, t % 4
                c0 = seg * SEGW + 1 + i * 512
                W(nc.gpsimd, s_ubu, t + 1)
                W(nc.gpsimd, s_ubb, t + 1)
                ub_tok[t] = [gq.dma(uTd[:, :, c0:c0 + 512], uT[:, p, :, :]),
                             gq.dma(bTd[:, :, row0:row0 + 512], bT[:, p, :, :])]

            for tk in qk_tok.get(t - 2, []):
                WT(nc.scalar, tk)
            def stage_A(g, s):
                q = g % 2
                tk = tokc[:, p, s, :]
                za, zb = ZQ[q]
                W(nc.tensor, s_ka, g - 1)
                if q == 1:
                    W(nc.tensor, s_frb[0], s_frb[0].n)
                    W(nc.tensor, s_frb[1], s_frb[1].n)
                for c in range(8):
                    nc.tensor.matmul(za[:, 0:384], lhsT=xnT[:, c, s * 128:(s + 1) * 128],
                                     rhs=win[:, c, 1536:1920], start=(c == 0), stop=(c == 7))
                for c in range(8):
                    ins = nc.tensor.matmul(zb[:, 0:288], lhsT=xnT[:, c, s * 128:(s + 1) * 128],
                                           rhs=win[:, c, 1920:2208], start=(c == 0), stop=(c == 7))
                s_z.sig(ins)
                W(nc.scalar, s_z, g + 1)
                s2 = st2[:, q, :]
                nc.scalar.activation(out=junk[:, 0:384], in_=za[:, 0:384], func=AF.Square, accum_out=s2[:, 0:1])
                fence(nc, nc.scalar, nc.scalar.activation(out=junk[:, 384:640], in_=zb[:, 0:256], func=AF.Square,
                                                          accum_out=s2[:, 1:2]))
                nc.scalar.activation(out=s2[:, 2:3], in_=s2[:, 0:1], func=AF.Sqrt, scale=1.0 / 384,
                                     bias=epst[:, 0:1])
                s_zs.sig(nc.scalar.activation(out=s2[:, 3:4], in_=s2[:, 1:2], func=AF.Sqrt, scale=1.0 / 256,
                                              bias=epst[:, 0:1]))
                W(nc.vector, s_zs, g + 1)
                W(nc.vector, s_tp, marks_tpk.get(g - 2, 0))
                fence(nc, nc.vector, nc.vector.reciprocal(out=s2[:, 4:6], in_=s2[:, 2:4]))
                nc.vector.scalar_tensor_tensor(out=qkn[:, q, 0:384], in0=za[:, 0:384], scalar=s2[:, 4:5],
                                               in1=gqn, op0=ALU.mult, op1=ALU.mult)
                nc.vector.scalar_tensor_tensor(out=qkn[:, q, 384:640], in0=zb[:, 0:256], scalar=s2[:, 5:6],
                                               in1=gkv, op0=ALU.mult, op1=ALU.mult)
                cs, sn = tk[:, 0:16], tk[:, 16:32]
                x1, x2 = zb[:, 256:272], zb[:, 272:288]
                kr = krt[:, q]
                nc.vector.tensor_tensor(out=kr[:, 0, :], in0=x1, in1=cs, op=ALU.mult)
                nc.vector.tensor_tensor(out=kr[:, 1, :], in0=x2, in1=sn, op=ALU.mult)
                nc.vector.tensor_tensor(out=kr[:, 2, :], in0=x2, in1=cs, op=ALU.mult)
                fence(nc, nc.vector, nc.vector.tensor_tensor(out=kr[:, 3, :], in0=x1, in1=sn, op=ALU.mult))
                nc.vector.tensor_tensor(out=krb[:, q, 0:16], in0=kr[:, 0, :], in1=kr[:, 1, :], op=ALU.subtract)
                fence(nc, nc.vector,
                      nc.vector.tensor_tensor(out=krb[:, q, 16:32], in0=kr[:, 2, :], in1=kr[:, 3, :], op=ALU.add))
                nc.vector.tensor_copy(out=Kb[:, q, :, 97:99], in_=tk[:, 34:36].unsqueeze(1).broadcast_to([128, 8, 2]))
                nc.vector.tensor_copy(out=Qb[:, q, :, 97:99], in_=tk[:, 32:34].unsqueeze(1).broadcast_to([128, 8, 2]))
                ins = nc.vector.tensor_copy(out=Kb[:, q, :, 64:96],
                                            in_=krb[:, q, :].unsqueeze(1).broadcast_to([128, 8, 32]))
                s_zn.sig(ins)

            def stage_B(g, s):
                q = g % 2
                W(nc.tensor, s_zn, g + 1)
                return tp_group([qkn[:, q, c * 128:(c + 1) * 128] for c in range(5)], qkT[:, q, :, :])

            def stage_C(g, s, tcn):
                q = g % 2
                tk = tokc[:, p, s, :]
                cs, sn = tk[:, 0:16], tk[:, 16:32]
                za, zb = ZQ[q]
                W(nc.tensor, s_tc, tcn)
                W(nc.tensor, s_qa, g)
                for c in range(3):
                    nc.tensor.matmul(qpa[:, 0:384], lhsT=qkT[:, q, c, :], rhs=wuq[:, c, 0:384],
                                     start=(c == 0), stop=(c == 2))
                for c in range(3):
                    ins = nc.tensor.matmul(qpb[:, 0:384], lhsT=qkT[:, q, c, :], rhs=wuq[:, c, 384:768],
                                           start=(c == 0), stop=(c == 2))
                s_q.sig(ins)
                for hh, kvb in enumerate((za, zb)):
                    for c in range(2):
                        ins = nc.tensor.matmul(kvb, lhsT=qkT[:, q, 3 + c, :],
                                               rhs=wukv[:, c, hh * 512:(hh + 1) * 512],
                                               start=(c == 0), stop=(c == 1))
                s_kv.sig(ins)
                W(nc.scalar, s_q, g + 1)
                W(nc.scalar, s_qd, g - 1)
                nc.scalar.copy(out=q_sb[:, q, 0:384], in_=qpa[:, 0:384])
                s_qa.sig(nc.scalar.copy(out=q_sb[:, q, 384:768], in_=qpb[:, 0:384]))
                W(nc.vector, s_qa, g + 1)
                q3 = q_sb[:, q, :].rearrange("p (h e) -> p h e", e=96)
                nc.vector.tensor_copy(out=Qb[:, q, :, 0:64], in_=q3[:, :, 0:64])
                qa, qb_ = q3[:, :, 64:80], q3[:, :, 80:96]
                cs8 = cs.unsqueeze(1).broadcast_to([128, 8, 16])
                sn8 = sn.unsqueeze(1).broadcast_to([128, 8, 16])
                r4 = rt[:, q]
                nc.vector.tensor_tensor(out=r4[:, 0], in0=qa, in1=cs8, op=ALU.mult)
                nc.vector.tensor_tensor(out=r4[:, 1], in0=qb_, in1=sn8, op=ALU.mult)
                nc.vector.tensor_tensor(out=r4[:, 2], in0=qb_, in1=cs8, op=ALU.mult)
                ins = nc.vector.tensor_tensor(out=r4[:, 3], in0=qa, in1=sn8, op=ALU.mult)
                fence(nc, nc.vector, ins)
                nc.vector.tensor_tensor(out=Qb[:, q, :, 64:80], in0=r4[:, 0], in1=r4[:, 1], op=ALU.subtract)
                s_qd.sig(nc.vector.tensor_tensor(out=Qb[:, q, :, 80:96], in0=r4[:, 2], in1=r4[:, 3], op=ALU.add))
                W(nc.scalar, s_kv, g + 1)
                W(nc.scalar, s_kd, g - 1)
                nc.scalar.copy(out=kv_sb[:, q, 0:512], in_=za)
                s_ka.sig(nc.scalar.copy(out=kv_sb[:, q, 512:1024], in_=zb))
                W(nc.vector, s_ka, g + 1)
                WT(nc.vector, v_tok.get(g - 2))
                kv3 = kv_sb[:, q, :].rearrange("p (h e) -> p h e", e=128)
                nc.vector.tensor_copy(out=Kb[:, q, :, 0:64], in_=kv3[:, :, 0:64])
                s_kd.sig(nc.vector.tensor_copy(out=Vb[:, q, :, 0:64], in_=kv3[:, :, 64:128]))
                W(nc.gpsimd, s_kd, g + 1)
                r0 = row0 + s * 128
                v_tok[g] = gq.dma(T["V_d"][r0:r0 + 128, :, :], Vb[:, q, :, :])

            def stage_D(g, s):
                q = g % 2
                W(nc.tensor, s_qd, g + 1)
                tp_group([Qb[:, q, h, :] for h in range(8)], QT[:, p, :, s * 128:(s + 1) * 128])
                W(nc.tensor, s_kd, g + 1)
                tp_group([Kb[:, q, h, :] for h in range(8)], KT[:, p, :, s * 128:(s + 1) * 128])
                marks_tpk[g] = s_tp.n

            for s0 in range(0, nsub, 2):
                ss = [s_ for s_ in (s0, s0 + 1) if s_ < nsub]
                gs = [G[0] + k for k in range(len(ss))]
                for g, s_ in zip(gs, ss):
                    stage_A(g, s_)
                for g, s_ in zip(gs, ss):
                    tcn = stage_B(g, s_)
                    stage_C(g, s_, tcn)
                for g, s_ in zip(gs, ss):
                    stage_D(g, s_)
                G[0] += len(ss)
                if s0 == 0 and t + 1 < len(tiles[:MIX_NT]):
                    emit_norm(t + 1)
            last_of_tile[t] = G[0]
            if MIX_SUB < 6:
                continue
            W(nc.gpsimd, s_tc, s_tc.n)
            qk_tok[t] = [gq.dma(KTd[:, :, row0:row0 + ntok], KT[:, p, :, 0:ntok])]
            if not is_meta:
                qk_tok[t].append(gq.dma(QTd[:, :, row0:row0 + 512], QT[:, p, :, :]))

        if MIX_STEPS < 4:
            for e in (nc.gpsimd, nc.sync, nc.scalar, nc.vector, nc.tensor):
                for sem in (s_sq, s_xn, s_tp, s_tc, s_fw, s_frb[0], s_frb[1], s_ubu, s_ubb, s_z, s_zs, s_zn, s_q, s_qa, s_qd, s_kv, s_ka,
                            s_kd, s_misc):
                    W(e, sem, sem.n)
                s_xl.wait_all(e, 0)
                s_xl.wait_all(e, 1)
                gq.drain(e)
            return
        a_, b_ = flg[:, 0:1], flg[:, 1:2]
        um = usave[:, :, 2:3]
        W(nc.vector, s_ubu, s_ubu.n)
        fence(nc, nc.vector, nc.vector.memset(hal, 0.0))
        fence(nc, nc.vector, nc.vector.tensor_scalar(out=hal[:, :, 2:3], in0=usave[:, :, 0:1], scalar1=a_,
                                                     scalar2=None, op0=ALU.mult))
        nc.vector.tensor_copy(out=hal[:, :, 0:1], in_=um)
        nc.vector.tensor_copy(out=hal[:, :, 4:5], in_=um)
        nc.vector.tensor_scalar(out=hal[:, :, 1:2], in0=usave[:, :, 1:2], scalar1=a_, scalar2=None, op0=ALU.mult)
        ins = nc.vector.scalar_tensor_tensor(out=hal[:, :, 2:3], in0=um, scalar=b_, in1=hal[:, :, 2:3],
                                             op0=ALU.mult, op1=ALU.add)
        s_misc.sig(ins)
        W(nc.gpsimd, s_misc, 2)
        cols = [0, SEGW - 1, SEGW, 2 * SEGW - 1, 2 * SEGW, 3 * SEGW - 1]
        for k, col in enumerate(cols):
            gq.dma(uTd[:, :, col:col + 1], hal[:, :, k:k + 1], allow_slow_non_contiguous=True)
        for e in (nc.gpsimd, nc.sync, nc.scalar, nc.vector, nc.tensor):
            gq.drain(e)


def attn_phase(nc, gq, PS, T, onesb):
    tp, ps = PS
    LMAX = 4096
    with ExitStack() as st:
        KT = sb(nc, st, "a_KT", [128, 8, LMAX + NMETA], BF16)
        V = sb(nc, st, "a_V", [128, LMAX // 128 + 1, 8, 65], BF16)
        QT = sb(nc, st, "a_QT", [128, 2, 8, 512], BF16)
        NSB = 2
        DEFER = 8
        SB = [ps[:, 0:2, :].rearrange("p a b -> p (a b)"), ps[:, 2:4, :].rearrange("p a b -> p (a b)")]
        bcb = tp[0].bitcast(F32).rearrange("p a b -> p (a b)")
        wob = tp[1].bitcast(F32).rearrange("p a b -> p (a b)")
        NPB = 3
        Pb = sb(nc, st, "a_P", [128, NPB, 1024], BF16)
        wc = sb(nc, st, "o_wc", [128, 4, D], BF16)
        wa = sb(nc, st, "o_wa", [64, 8, D], BF16)
        cw = sb(nc, st, "o_cw", [128, 3, 4], F32)
        uT = sb(nc, st, "o_uT", [128, 4, 514], BF16)
        bT = sb(nc, st, "o_bT", [128, 4, 512], BF16)
        x1 = sb(nc, st, "o_x1", [128, 4, D], F32)
        yc = sb(nc, st, "o_yc", [128, 4, 512], BF16)
        acc = sb(nc, st, "o_acc", [128, 2, 512], F32)
        s_ol = Sem(nc, "o_l")
        s_oc = Sem(nc, "o_c")
        s_om = Sem(nc, "o_m")
        s_or = Sem(nc, "o_r")
        wo = T["w_out"]
        owt = [gq.dma(wc[:, n, :], wo[n * 128:(n + 1) * 128, :]) for n in range(4)]
        owt.append(gq.dma(wa, wo[512:1024, :].rearrange("(h p) d -> p h d", p=64)))
        owt.append(gq.dma(cw, T["conv_wT"]))
        for tk in owt:
            WT(nc.vector, tk)
            WT(nc.tensor, tk)
        uTd = T["uT_d"].rearrange("n p c -> p n c")
        bTd = T["bT_d"].rearrange("n p c -> p n c")
        owq = []
        ost = {}
        OC = [0]

        def o_loads(tile):
            row0 = tile * 512
            c0 = (tile // 4) * SEGW + (tile % 4) * 512
            WT(nc.sync, ost.get(tile - 1))
            s_ol.sig(nc.sync.dma_start(out=uT, in_=uTd[:, :, c0:c0 + 514]), 16)
            s_ol.sig(nc.sync.dma_start(out=bT, in_=bTd[:, :, row0:row0 + 512]), 16)
            s_ol.sig(nc.sync.dma_start(
                out=x1, in_=T["x1_d"][row0:row0 + 512, :].rearrange("(s p) d -> p s d", p=128)), 16)

        def o_conv(tile, n):
            if n == 0:
                W(nc.vector, s_ol, 48 * (tile + 1))
                W(nc.vector, s_om, 8 * tile)
            k = n % 2
            fence(nc, nc.vector, nc.vector.tensor_scalar(out=acc[:, k, :], in0=uT[:, n, 1:513],
                                                         scalar1=cw[:, 1, n:n + 1], scalar2=None, op0=ALU.mult))
            fence(nc, nc.vector, nc.vector.scalar_tensor_tensor(out=acc[:, k, :], in0=uT[:, n, 0:512],
                                                                scalar=cw[:, 0, n:n + 1], in1=acc[:, k, :],
                                                                op0=ALU.mult, op1=ALU.add))
            fence(nc, nc.vector, nc.vector.scalar_tensor_tensor(out=acc[:, k, :], in0=uT[:, n, 2:514],
                                                                scalar=cw[:, 2, n:n + 1], in1=acc[:, k, :],
                                                                op0=ALU.mult, op1=ALU.add))
            s_oc.sig(nc.vector.tensor_tensor(out=yc[:, n, :], in0=acc[:, k, :], in1=bT[:, n, :], op=ALU.mult))

        def o_mm(tile, qc, s_, n2, k):
            m = OC[0]
            if k == 0:
                W(nc.tensor, s_oc, 4 * (tile + 1))
                W(nc.tensor, s_nm, 8 * (qc + 1))
                W(nc.tensor, s_or, m)
            if k < 4:
                ins = nc.tensor.matmul(wob, lhsT=yc[:, k, s_ * 128:(s_ + 1) * 128],
                                       rhs=wc[:, k, n2 * 512:(n2 + 1) * 512], start=(k == 0), stop=False)
            else:
                h = k - 4
                ins = nc.tensor.matmul(wob, lhsT=yT[:, qc % 2, h, s_ * 128:(s_ + 1) * 128],
                                       rhs=wa[:, h, n2 * 512:(n2 + 1) * 512], start=False, stop=(h == 7))
            if k < 11:
                return
            s_om.sig(ins)
            W(nc.vector, s_om, m + 1)
            xs = x1[:, s_, n2 * 512:(n2 + 1) * 512]
            s_or.sig(nc.vector.tensor_tensor(out=xs, in0=wob, in1=xs, op=ALU.add))
            OC[0] += 1
            if s_ == 3 and n2 == 1:
                row0 = tile * 512
                W(nc.gpsimd, s_or, OC[0])
                ost[tile] = gq.dma(T["x2_d"][row0:row0 + 512, :].rearrange("(s p) d -> p s d", p=128), x1)

        def schedule_out(tile, qc, at, sp):
            items = [lambda n=n: o_conv(tile, n) for n in range(4)]
            for s_ in range(4):
                for n2 in range(2):
                    for k0 in (0, 6):
                        items.append(lambda s_=s_, n2=n2, k0=k0: [o_mm(tile, qc, s_, n2, k) for k in range(k0, k0 + 6)])
            if tile + 1 < 12:
                items.append(lambda: o_loads(tile + 1))
            for k, it in enumerate(items):
                owq.append((at + k * sp, it))

        o_loads(0)
        yT = sb(nc, st, "a_yT", [64, 2, 8, 512], BF16)
        rec = sb(nc, st, "a_rec", [128, 2, 512], F32)
        rhl = sb(nc, st, "a_rhl", [128, 2, 2, 512], BF16)
        bcs = sb(nc, st, "a_bcs", [64, 2, 512], F32)

        s_kv = Sem(nc, "a_kv")
        s_kv2 = Sem(nc, "a_kv2")
        s_ql = PSem(nc, "a_ql")
        s_s = Sem(nc, "a_s")
        s_e = Sem(nc, "a_e")
        s_pv = Sem(nc, "a_pv")
        s_rc = Sem(nc, "a_rc")
        s_bc = Sem(nc, "a_bc")
        s_bs = Sem(nc, "a_bs")
        s_nm = Sem(nc, "a_nm")
        y_tok = {}

        KTd = T["KT_d"].rearrange("h p c -> p h c")
        QTd = T["QT_d"].rearrange("h p c -> p h c")
        yTd = T["yT_d"].rearrange("h p c -> p h c")
        GB = [0]
        I0 = 0
        HC0 = 0
        QC0 = 0
        for slot, (tok0, L) in enumerate([(0, 4096), (4096, 2048)]):
            nkt = L // 128 + 1
            nqt = L // 512
            W(nc.sync, s_pv, I0)
            W(nc.scalar, s_pv, I0)
            s_kv.sig(nc.sync.dma_start(out=KT[:, 0, 0:L], in_=KTd[:, 0, tok0:tok0 + L]), 16)
            s_kv.sig(nc.sync.dma_start(out=KT[:, :, L:L + NMETA], in_=KTd[:, :, MROW:MROW + NMETA]), 16)
            s_kv.sig(nc.sync.dma_start(out=V[0:NMETA, nkt - 1, :, :], in_=T["V_d"][MROW:MROW + NMETA, :, :]), 16)
            for k0 in range(0, nkt - 1, 8):
                s_kv.sig(nc.scalar.dma_start(
                    out=V[:, k0:k0 + 8, :, :],
                    in_=T["V_d"][tok0 + k0 * 128:tok0 + (k0 + 8) * 128, :, :].rearrange(
                        "(k p) h d -> p k h d", p=128)), 16)
            for h in range(1, 8):
                eng = nc.sync if h % 2 == 1 else nc.scalar
                s_kv2.sig(eng.dma_start(out=KT[:, h, 0:L], in_=KTd[:, h, tok0:tok0 + L]), 16)
            kv2_target = s_kv2.n
            W(nc.tensor, s_kv, s_kv.n)

            upq = []
            for kt0 in range(0, nkt - 1, 2):
                upq.append((kt0, 2))
            upq.append((nkt - 1, 1))
            nuh = len(upq)
            blocks = [(qt, h, kt0, nk) for qt in range(nqt) for h in range(8) for (kt0, nk) in upq]
            nb = len(blocks)

            def load_q(qt):
                qc = QC0 + qt
                if qc >= 2:
                    W(nc.sync, s_pv, I0 + (qt - 1) * 8 * nuh if qt >= 1 else I0)
                s_ql.sig(qc % 2, nc.sync.dma_start(out=QT[:, qc % 2, :, :],
                                                   in_=QTd[:, :, tok0 + qt * 512:tok0 + (qt + 1) * 512]))

            def emit_S(bi):
                qt, h, kt0, nk = blocks[bi]
                kw = 128 if kt0 < nkt - 1 else NMETA
                gi = I0 + bi
                qc = QC0 + qt
                if h == 0 and kt0 == 0:
                    s_ql.wait_all(nc.tensor, qc % 2)
                    if qt + 1 < nqt:
                        load_q(qt + 1)
                if qt == 0 and h == 1 and kt0 == 0:
                    W(nc.tensor, s_kv2, kv2_target)
                sbank = SB[gi % NSB]
                for j in range(nk):
                    kt = kt0 + j
                    ins = nc.tensor.matmul(sbank[0:kw, j * 512:(j + 1) * 512],
                                           lhsT=KT[:, h, kt * 128:kt * 128 + kw],
                                           rhs=QT[:, qc % 2, h, :], start=True, stop=True)
                    if j == 0:
                        WI(ins, s_e, gi - (NSB - 1))
                s_s.sig(ins)
                ins = nc.scalar.activation(out=Pb[0:kw, gi % NPB, 0:nk * 512], in_=sbank[0:kw, 0:nk * 512],
                                           func=AF.Exp, scale=SCALE)
                WI(ins, s_s, gi + 1)
                s_e.sig(ins)

            pending = []

            def emit_PV(bi):
                qt, h, kt0, nk = blocks[bi]
                kw = 128 if kt0 < nkt - 1 else NMETA
                gi = I0 + bi
                hc = HC0 + (bi // nuh)
                qc = QC0 + qt
                ob = ps[:, 4 + hc % 2, :]
                last = (kt0 + nk == nkt)
                if kt0 == 0:
                    W(nc.tensor, s_nm, hc - 1)
                for j in range(nk):
                    kt = kt0 + j
                    ins = nc.tensor.matmul(ob[0:65, :], lhsT=V[0:kw, kt, h, :],
                                           rhs=Pb[0:kw, gi % NPB, j * 512:(j + 1) * 512],
                                           start=(kt == 0), stop=(kt == nkt - 1))
                    if j == 0:
                        WI(ins, s_e, gi + 1)
                s_pv.sig(ins)
                if last:
                    W(nc.vector, s_pv, gi + 1)
                    fence(nc, nc.vector, nc.vector.reciprocal(out=rec[64:65, hc % 2, :], in_=ob[64:65, :]))
                    fence(nc, nc.vector, nc.vector.tensor_copy(out=rhl[64:65, hc % 2, 0, :], in_=rec[64:65, hc % 2, :]))
                    s_rc.sig(nc.vector.tensor_tensor(out=rhl[64:65, hc % 2, 1, :], in0=rec[64:65, hc % 2, :],
                                                     in1=rhl[64:65, hc % 2, 0, :], op=ALU.subtract))
                    pending.append((bi + DEFER, hc, h, qt, qc))

            def emit_norm(hc, h, qt, qc):
                ob = ps[:, 4 + hc % 2, :]
                W(nc.tensor, s_rc, hc + 1)
                W(nc.tensor, s_bs, hc)
                nc.tensor.matmul(bcb[0:64, :], lhsT=onesb[64:65, 0:64], rhs=rhl[64:65, hc % 2, 0, :],
                                 start=True, stop=False)
                s_bc.sig(nc.tensor.matmul(bcb[0:64, :], lhsT=onesb[64:65, 0:64], rhs=rhl[64:65, hc % 2, 1, :],
                                          start=False, stop=True))
                W(nc.scalar, s_bc, hc + 1)
                W(nc.scalar, s_nm, hc - 1)
                s_bs.sig(nc.scalar.copy(out=bcs[:, hc % 2, :], in_=bcb[0:64, :]))
                W(nc.vector, s_bs, hc + 1)
                if h == 0:
                    W(nc.vector, s_om, 8 * max(0, qc - 1))
                s_nm.sig(nc.vector.tensor_tensor(out=yT[:, qc % 2, h, :], in0=ob[0:64, :],
                                                 in1=bcs[:, hc % 2, :], op=ALU.mult))
                if h == 7:
                    if DEBUG:
                        W(nc.gpsimd, s_nm, hc + 1)
                        gq.dma(yTd[:, :, tok0 + qt * 512:tok0 + (qt + 1) * 512], yT[:, qc % 2, :, :])
                    schedule_out(qc, qc, GB[0] + 1, 6 if qc < 7 else 3)

            if QC0 >= 2:
                W(nc.sync, s_pv, I0)
            s_ql.sig(QC0 % 2, nc.sync.dma_start(out=QT[:, QC0 % 2, :, :], in_=QTd[:, :, tok0:tok0 + 512]))
            for bi in range(min(NSB, nb)):
                emit_S(bi)
            for bi in range(nb):
                GB[0] = I0 + bi
                if bi + NSB < nb:
                    emit_S(bi + NSB)
                emit_PV(bi)
                while pending and pending[0][0] <= bi:
                    emit_norm(*pending.pop(0)[1:])
                while owq and owq[0][0] <= GB[0]:
                    owq.pop(0)[1]()
            while pending:
                emit_norm(*pending.pop(0)[1:])
            I0 += nb
            HC0 += nb // nuh
            QC0 += nqt
        while owq:
            owq.pop(0)[1]()
        for e in (nc.gpsimd, nc.sync, nc.scalar, nc.vector, nc.tensor):
            gq.drain(e)


def out_phase(nc, gq, PS, T):
    tp, ps = PS
    with ExitStack() as st:
        wc = sb(nc, st, "o_wc", [128, 4, D], BF16)
        wa = sb(nc, st, "o_wa", [64, 8, D], BF16)
        cw = sb(nc, st, "o_cw", [128, 3, 4], F32)
        uT = sb(nc, st, "o_uT", [128, 2, 4, 514], BF16)
        bT = sb(nc, st, "o_bT", [128, 2, 4, 512], BF16)
        yT = sb(nc, st, "o_yT", [64, 2, 8, 512], BF16)
        x1 = sb(nc, st, "o_x1", [128, 2, 4, D], F32)
        yc = sb(nc, st, "o_yc", [128, 2, 4, 512], BF16)
        acc = sb(nc, st, "o_acc", [128, 2, 512], F32)

        s_l = PSem(nc, "o_l")
        s_c = Sem(nc, "o_c")
        s_m = Sem(nc, "o_m")
        s_r = Sem(nc, "o_r")

        wo = T["w_out"]
        wt = [gq.dma(wc[:, n, :], wo[n * 128:(n + 1) * 128, :]) for n in range(4)]
        wt.append(gq.dma(wa, wo[512:1024, :].rearrange("(h p) d -> p h d", p=64)))
        wt.append(gq.dma(cw, T["conv_wT"]))
        for tk in wt:
            WT(nc.vector, tk)
            WT(nc.tensor, tk)
        st_tok = {}

        uTd = T["uT_d"].rearrange("n p c -> p n c")
        bTd = T["bT_d"].rearrange("n p c -> p n c")
        yTd = T["yT_d"].rearrange("h p c -> p h c")
        M = [0]
        for t in range(12):
            p = t % 2
            row0 = t * 512
            seg, i = t // 4, t % 4
            c0 = seg * SEGW + i * 512
            WT(nc.sync, st_tok.get(t - 2))
            s_l.sig(p, nc.sync.dma_start(out=uT[:, p, :, :], in_=uTd[:, :, c0:c0 + 514]))
            s_l.sig(p, nc.sync.dma_start(out=bT[:, p, :, :], in_=bTd[:, :, row0:row0 + 512]))
            s_l.sig(p, nc.sync.dma_start(out=yT[:, p, :, :], in_=yTd[:, :, row0:row0 + 512]))
            s_l.sig(p, nc.sync.dma_start(
                out=x1[:, p, :, :], in_=T["x1_d"][row0:row0 + 512, :].rearrange("(s p) d -> p s d", p=128)))
            s_l.wait_all(nc.vector, p)
            if t >= 2:
                W(nc.vector, s_m, 8 * (t - 1))
            for n0 in (0, 2):
                for k in range(2):
                    n = n0 + k
                    nc.vector.tensor_scalar(out=acc[:, k, :], in0=uT[:, p, n, 1:513], scalar1=cw[:, 1, n:n + 1],
                                            scalar2=None, op0=ALU.mult)
                for k in range(2):
                    n = n0 + k
                    nc.vector.scalar_tensor_tensor(out=acc[:, k, :], in0=uT[:, p, n, 0:512],
                                                   scalar=cw[:, 0, n:n + 1], in1=acc[:, k, :],
                                                   op0=ALU.mult, op1=ALU.add)
                for k in range(2):
                    n = n0 + k
                    nc.vector.scalar_tensor_tensor(out=acc[:, k, :], in0=uT[:, p, n, 2:514],
                                                   scalar=cw[:, 2, n:n + 1], in1=acc[:, k, :],
                                                   op0=ALU.mult, op1=ALU.add)
                for k in range(2):
                    n = n0 + k
                    ins = nc.vector.tensor_tensor(out=yc[:, p, n, :], in0=acc[:, k, :], in1=bT[:, p, n, :],
                                                  op=ALU.mult)
            s_c.sig(ins)
            W(nc.tensor, s_c, t + 1)
            s_l.wait_all(nc.tensor, p)
            for s in range(4):
                for n2 in range(2):
                    m = M[0]
                    W(nc.tensor, s_r, m - 1)
                    bank = ps[:, m % 2, :]
                    for n in range(4):
                        nc.tensor.matmul(bank, lhsT=yc[:, p, n, s * 128:(s + 1) * 128],
                                         rhs=wc[:, n, n2 * 512:(n2 + 1) * 512], start=(n == 0), stop=False)
                    for h in range(8):
                        ins = nc.tensor.matmul(bank, lhsT=yT[:, p, h, s * 128:(s + 1) * 128],
                                               rhs=wa[:, h, n2 * 512:(n2 + 1) * 512], start=False, stop=(h == 7))
                    s_m.sig(ins)
                    W(nc.vector, s_m, m + 1)
                    xs = x1[:, p, s, n2 * 512:(n2 + 1) * 512]
                    s_r.sig(nc.vector.tensor_tensor(out=xs, in0=bank, in1=xs, op=ALU.add))
                    M[0] += 1
            W(nc.gpsimd, s_r, M[0])
            st_tok[t] = gq.dma(T["x2_d"][row0:row0 + 512, :].rearrange("(s p) d -> p s d", p=128), x1[:, p, :, :])
        for e in (nc.gpsimd, nc.sync, nc.scalar, nc.vector, nc.tensor):
            gq.drain(e)


def build_nc():
    nc = bass.Bass("TRN2", target_bir_lowering=False)
    free = sorted(nc.free_semaphores)
    nc.gpsimd.sem_clear(range(free[0], free[-1] + 1))
    nc.gpsimd.dma_reset()
    nc.all_engine_barrier()
    gq = GQ(nc, 12)
    T = {}

    def din(name, shape):
        T[name] = nc.dram_tensor(name, list(shape), F32, kind="ExternalInput").ap()

    din("xin", [NROW, D])
    din("tokc", [NROW, NCONST])
    din("flags", [128, 2])
    for f in ("ffn1", "ffn2"):
        din(f + "_norm", [D])
        din(f + "_w_gate", [D, DFF])
        din(f + "_w_up", [D, DFF])
        din(f + "_w_down", [DFF, D])
    din("mix_norm", [D])
    din("w_in", [D, DIN])
    din("conv_wT", [128, 3, 4])
    din("q_norm", [384])
    din("w_uq", [384, 768])
    din("kv_norm", [256])
    din("w_ukv", [256, 1024])
    din("w_out", [D, D])
    din("final_norm", [D])
    T["y"] = nc.dram_tensor("y", [NTOK, D], F32, kind="ExternalOutput").ap()
    kind = "ExternalOutput" if DEBUG else "Internal"

    def scr(name, shape, dt):
        T[name] = nc.dram_tensor(name, list(shape), dt, kind=kind).ap()

    scr("x1_d", [NROW, D], F32)
    scr("uT_d", [4, 128, 3 * SEGW], BF16)
    scr("bT_d", [4, 128, NTOK], BF16)
    scr("QT_d", [8, 128, NTOK], BF16)
    scr("KT_d", [8, 128, NROW], BF16)
    scr("V_d", [NROW, 8, 65], BF16)
    scr("yT_d", [8, 64, NTOK], BF16)
    scr("x2_d", [NTOK, D], F32)

    tp = [nc.alloc_psum_tensor("tp%d" % i, [128, 8, 128], BF16).ap() for i in range(2)]
    ps = nc.alloc_psum_tensor("ps", [128, 6, 512], F32).ap()
    PS = (tp, ps)
    ident = nc.alloc_sbuf_tensor("ident", [128, 128], BF16).ap()
    identf = nc.alloc_sbuf_tensor("identf", [128, 128], F32).ap()
    onesf = nc.alloc_sbuf_tensor("onesf", [128, 64], F32).ap()
    onesb = nc.alloc_sbuf_tensor("onesb", [128, 64], BF16).ap()
    epst = nc.alloc_sbuf_tensor("epst", [128, 1], F32).ap()
    s0 = Sem(nc, "init")
    fence(nc, nc.gpsimd, nc.gpsimd.memset(identf, 0.0))
    fence(nc, nc.gpsimd, nc.gpsimd.affine_select(out=identf, in_=identf, pattern=[[-1, 128]],
                                                 compare_op=ALU.not_equal, fill=1.0, base=0, channel_multiplier=1))
    nc.gpsimd.tensor_copy(out=ident, in_=identf)
    nc.gpsimd.memset(onesf, 1.0)
    nc.gpsimd.memset(onesb, 1.0)
    s0.sig(nc.gpsimd.memset(epst, EPS))
    for e in (nc.sync, nc.scalar, nc.vector, nc.tensor):
        W(e, s0, 1)

    tiles13 = [(t * 512, 4) for t in range(12)] + [(MROW, 1)]
    tiles12 = [(t * 512, 4) for t in range(12)]
    ffn_phase(nc, gq, PS, ident, epst, "f1", T["xin"], tiles13, T["ffn1_w_gate"], T["ffn1_w_up"], T["ffn1_w_down"],
              T["ffn1_norm"], T["x1_d"])
    if PHASES >= 2:
        mix_phase(nc, gq, PS, ident, epst, T)
    if PHASES >= 3:
        attn_phase(nc, gq, PS, T, onesb)
    if PHASES >= 5:
        ffn_phase(nc, gq, PS, ident, epst, "f2", T["x2_d"], tiles12, T["ffn2_w_gate"], T["ffn2_w_up"], T["ffn2_w_down"],
                  T["ffn2_norm"], T["y"], gF_d=T["final_norm"])
    return nc


def _core_layout(c):
    if c < 4:
        return [("p", c)], [("s", c)]
    i = c - 4
    return [("s", 4 + 3 * i), ("s", 4 + 3 * i + 1)], [("s", 4 + 3 * i + 2)]


def _consts(c):
    f32 = np.float32
    inv_freq = (1.0 / (f32(10000.0) ** (np.arange(0, 32, 2, dtype=f32) / f32(32)))).astype(f32)
    pos = np.zeros(NROW, f32)
    tk = np.zeros((NROW, NCONST), f32)
    if c < 4:
        pos[0:4096] = 16 + np.arange(4096)
        flags = (1.0, 0.0)
    else:
        pos[0:2048] = 16 + np.arange(2048)
        pos[2048:4096] = 16 + np.arange(2048)
        tk[0:2048, 32] = 128.0
        tk[2048:4096, 33] = 128.0
        tk[0:2048, 35] = -128.0
        tk[2048:4096, 34] = -128.0
        flags = (0.0, 1.0)
    pos[4096:6144] = 16 + np.arange(2048)
    pos[MROW:MROW + NMETA] = np.arange(NMETA)
    ang = (pos[:, None] * inv_freq[None, :]).astype(f32)
    tk[:, 0:16] = np.cos(ang).astype(f32)
    tk[:, 16:32] = np.sin(ang).astype(f32)
    fl = np.zeros((128, 2), f32)
    fl[:, 0] = flags[0]
    fl[:, 1] = flags[1]
    return tk, fl


def make_in_maps(inputs):
    f32 = np.float32
    xp = np.asarray(inputs["x_prompt"], f32)
    xs = np.asarray(inputs["x_sample"], f32)
    meta = np.asarray(inputs["meta_tokens"], f32)
    shared = {}
    for f in ("ffn1", "ffn2"):
        shared[f + "_norm"] = np.ascontiguousarray(np.asarray(inputs[f + "_norm"], f32)[0])
        for w in ("_w_gate", "_w_up", "_w_down"):
            shared[f + w] = np.ascontiguousarray(np.asarray(inputs[f + w], f32)[0])
    for k in ("mix_norm", "w_in", "q_norm", "w_uq", "kv_norm", "w_ukv", "w_out"):
        shared[k] = np.ascontiguousarray(np.asarray(inputs[k], f32)[0])
    cwt = np.asarray(inputs["conv_w"], f32)[0]
    shared["conv_wT"] = np.ascontiguousarray(cwt.reshape(3, 4, 128).transpose(2, 0, 1))
    shared["final_norm"] = np.ascontiguousarray(np.asarray(inputs["final_norm"], f32))
    maps = []
    for c in range(8):
        s0, s1 = _core_layout(c)
        xin = np.zeros((NROW, D), f32)
        r = 0
        for which, idx in s0 + s1:
            src = xp[idx] if which == "p" else xs[idx]
            xin[r:r + src.shape[0]] = src
            r += src.shape[0]
        xin[MROW:MROW + NMETA] = meta
        tk, fl = _consts(c)
        m = dict(shared)
        m["xin"] = xin
        m["tokc"] = tk
        m["flags"] = fl
        maps.append(m)
    return maps


def kernel(**inputs):
    nc = build_nc()
    maps = make_in_maps(inputs)
    res = run_bass_kernel_spmd(nc, maps, core_ids=list(range(8)))
    yp = np.zeros((4, 4096, D), np.float32)
    ys = np.zeros((16, 2048, D), np.float32)
    for c in range(8):
        y = np.asarray(res.results[c]["y"], np.float32)
        s0, s1 = _core_layout(c)
        r = 0
        for which, idx in s0 + s1:
            if which == "p":
                yp[idx] = y[r:r + 4096]
                r += 4096
            else:
                ys[idx] = y[r:r + 2048]
                r += 2048
    return (yp, ys)
```
